# Optimizing a Trainium2 kernel written in Bass

```python
import jax, jax.numpy as jnp
from jax import lax
import numpy as np

D_MODEL = 2048
BATCH = 4
SEQ = 2048
DEPTH = 2
DEC_BATCH = 32
DEC_SEQ = 32
PAST_LEN = 2048

CHUNK = 64
N_AB_LAYERS = (DEPTH + 1) // 2
N_ATT_LAYERS = DEPTH // 2
RMS_EPS = 1e-6

MIX_WIDTH = D_MODEL
H_RET = 4
DV_RET = MIX_WIDTH // 2 // H_RET
DK_RET = DV_RET
H_GLA = 4
DV_GLA = MIX_WIDTH // 2 // H_GLA
DK_GLA = DV_GLA // 2
GLA_GATE_RANK = 16
GLA_GATE_TAU = 16.0
ROPE_BASE = 10000.0
AB_SPLIT = (H_RET * DK_RET, H_RET * DK_RET, H_RET * DV_RET, H_RET * DV_RET,
            H_GLA * DK_GLA, H_GLA * DK_GLA, H_GLA * DV_GLA, H_GLA * DV_GLA, GLA_GATE_RANK)
AB_IN_WIDTH = 2 * H_RET * DK_RET + 2 * H_RET * DV_RET + 2 * H_GLA * DK_GLA + 2 * H_GLA * DV_GLA + GLA_GATE_RANK

H_ATT = 16
DH_ATT = MIX_WIDTH // H_ATT
LEFT_CHUNKS = 8
MAX_REL = 256
NEG_INF = -1e30

D_FF = 5632
CONV_W = 3

kernel_name = "hybrid_streaming_retention_gla_chunkattn_convffn_step"


def rmsnorm(x, g):
    xf = x.astype(jnp.float32)
    y = xf * lax.rsqrt(jnp.mean(xf * xf, axis=-1, keepdims=True) + RMS_EPS)
    return (y * g.astype(jnp.float32)).astype(x.dtype)


def head_rmsnorm(o, g):
    return o * lax.rsqrt(jnp.mean(o * o, axis=-1, keepdims=True) + RMS_EPS) * g.astype(jnp.float32)


def rope(x, pos):
    half = x.shape[-1] // 2
    inv = ROPE_BASE ** (-jnp.arange(half, dtype=jnp.float32) / half)
    ang = pos.astype(jnp.float32)[:, None] * inv[None, :]
    cos = jnp.cos(ang)[None, :, None, :]
    sin = jnp.sin(ang)[None, :, None, :]
    x1, x2 = x[..., :half], x[..., half:]
    return jnp.concatenate([x1 * cos - x2 * sin, x1 * sin + x2 * cos], axis=-1)


def retention_chunk(S, q, k, v, log_gamma):
    C = q.shape[1]
    idx = jnp.arange(C, dtype=jnp.float32)
    diff = idx[:, None] - idx[None, :]
    causal = (diff >= 0)[None]
    decay = jnp.where(causal, jnp.exp(jnp.where(causal, diff[None], 0.0) * log_gamma[:, None, None]), 0.0)
    scores = jnp.einsum('bihd,bjhd->bhij', q, k) * decay[None]
    o = jnp.einsum('bhij,bjhv->bihv', scores, v)
    q_dec = jnp.exp((idx + 1.0)[None, :] * log_gamma[:, None])
    o = o + jnp.einsum('bihd,hi,bhdv->bihv', q, q_dec, S)
    k_dec = jnp.exp((C - 1.0 - idx)[None, :] * log_gamma[:, None])
    S_new = jnp.exp(C * log_gamma)[None, :, None, None] * S + jnp.einsum('bjhd,hj,bjhv->bhdv', k, k_dec, v)
    return S_new, o


def gla_chunk(S, q, k, v, log_a):
    C = q.shape[1]
    b = jnp.cumsum(log_a, axis=1)
    causal = jnp.tril(jnp.ones((C, C), dtype=bool))[None, :, :, None, None]
    diff = b[:, :, None] - b[:, None, :]
    w = jnp.where(causal, jnp.exp(jnp.where(causal, diff, 0.0)), 0.0)
    scores = jnp.einsum('bihd,bjhd,bijhd->bhij', q, k, w)
    o = jnp.einsum('bhij,bjhv->bihv', scores, v)
    o = o + jnp.einsum('bihd,bhdv->bihv', q * jnp.exp(b), S)
    b_last = b[:, -1]
    S_new = jnp.exp(b_last)[..., None] * S + jnp.einsum('bjhd,bjhv->bhdv', k * jnp.exp(b_last[:, None] - b), v)
    return S_new, o


def chunk_scan(step, S0, *xs):
    def to_chunks(a):
        B, T = a.shape[:2]
        return jnp.moveaxis(a.reshape((B, T // CHUNK, CHUNK) + a.shape[2:]), 1, 0)
    S, o = lax.scan(lambda S, c: step(S, *c), S0, tuple(to_chunks(a) for a in xs))
    o = jnp.moveaxis(o, 0, 1)
    return S, o.reshape((o.shape[0], -1) + o.shape[3:])


def ab_mixer(h, pos, s_ret, s_gla, w_in, gate_w2, gate_b, ret_g, gla_g, w_out):
    B, T, _ = h.shape
    f32 = jnp.float32
    z = jnp.einsum('btd,de->bte', h, w_in).astype(f32)
    cuts = [int(c) for c in np.cumsum(AB_SPLIT)[:-1]]
    qa, ka, va, ga, qb, kb, vb, gb, lo = jnp.split(z, cuts, axis=-1)
    qa = rope(qa.reshape(B, T, H_RET, DK_RET), pos)
    ka = rope(ka.reshape(B, T, H_RET, DK_RET), pos) * (DK_RET ** -0.5)
    va = va.reshape(B, T, H_RET, DV_RET)
    log_gamma = jnp.log1p(-jnp.exp2(-5.0 - jnp.arange(H_RET, dtype=f32)))
    qb = qb.reshape(B, T, H_GLA, DK_GLA) * (DK_GLA ** -0.5)
    kb = kb.reshape(B, T, H_GLA, DK_GLA)
    vb = vb.reshape(B, T, H_GLA, DV_GLA)
    log_a = (jax.nn.log_sigmoid(lo @ gate_w2.astype(f32) + gate_b.astype(f32)) / GLA_GATE_TAU).reshape(B, T, H_GLA, DK_GLA)
    if s_ret is None:
        s_ret0 = jnp.zeros((B, H_RET, DK_RET, DV_RET), f32)
        s_gla0 = jnp.zeros((B, H_GLA, DK_GLA, DV_GLA), f32)
        s_ret_new, o_ret = chunk_scan(lambda S, q, k, v: retention_chunk(S, q, k, v, log_gamma), s_ret0, qa, ka, va)
        s_gla_new, o_gla = chunk_scan(gla_chunk, s_gla0, qb, kb, vb, log_a)
    else:
        s_ret_new, o_ret = retention_chunk(s_ret.astype(f32), qa, ka, va, log_gamma)
        s_gla_new, o_gla = gla_chunk(s_gla.astype(f32), qb, kb, vb, log_a)
    o_ret = head_rmsnorm(o_ret, ret_g).reshape(B, T, -1) * jax.nn.silu(ga)
    o_gla = head_rmsnorm(o_gla, gla_g).reshape(B, T, -1) * jax.nn.silu(gb)
    o = jnp.concatenate([o_ret, o_gla], axis=-1).astype(h.dtype)
    return jnp.einsum('bte,ed->btd', o, w_out), s_ret_new, s_gla_new


def rel_bias_lookup(rel_bias, rel):
    return rel_bias.astype(jnp.float32)[:, jnp.clip(rel, -MAX_REL, MAX_REL) + MAX_REL]


def band_attention_prompt(q, k, v, rel_bias):
    B, T, H, dh = q.shape
    nc = T // CHUNK
    pad = LEFT_CHUNKS * CHUNK
    band = pad + CHUNK
    kp = jnp.pad(k, ((0, 0), (pad, 0), (0, 0), (0, 0)))
    vp = jnp.pad(v, ((0, 0), (pad, 0), (0, 0), (0, 0)))
    qc = q.reshape(B, nc, CHUNK, H, dh)
    i = jnp.arange(CHUNK)
    j = jnp.arange(band)
    bias = rel_bias_lookup(rel_bias, (i[:, None] + pad) - j[None, :])
    scale = dh ** -0.5

    def one_chunk(n):
        ks = lax.dynamic_slice_in_dim(kp, n * CHUNK, band, axis=1)
        vs = lax.dynamic_slice_in_dim(vp, n * CHUNK, band, axis=1)
        qn = lax.dynamic_index_in_dim(qc, n, axis=1, keepdims=False)
        s = jnp.einsum('bihd,bjhd->bhij', qn, ks) * scale + bias[None]
        valid = (n * CHUNK - pad + j) >= 0
        s = jnp.where(valid[None, None, None, :], s, NEG_INF)
        p = jax.nn.softmax(s, axis=-1)
        return jnp.einsum('bhij,bjhd->bihd', p, vs)

    o = lax.map(one_chunk, jnp.arange(nc))
    return jnp.moveaxis(o, 0, 1).reshape(B, T, H, dh)


def band_attention_sample(q, k_new, v_new, k_cache, v_cache, rel_bias):
    T = q.shape[1]
    Wc = k_cache.shape[1]
    ks = jnp.concatenate([k_cache.astype(jnp.float32), k_new], axis=1)
    vs = jnp.concatenate([v_cache.astype(jnp.float32), v_new], axis=1)
    q_pos = PAST_LEN + jnp.arange(T)
    key_pos = jnp.concatenate([PAST_LEN - Wc + jnp.arange(Wc), PAST_LEN + jnp.arange(T)])
    bias = rel_bias_lookup(rel_bias, q_pos[:, None] - key_pos[None, :])
    s = jnp.einsum('bihd,bjhd->bhij', q, ks) * (q.shape[-1] ** -0.5) + bias[None]
    p = jax.nn.softmax(s, axis=-1)
    return jnp.einsum('bhij,bjhd->bihd', p, vs)


def c_mixer(h, k_cache, v_cache, w_qkv, rel_bias, w_out):
    B, T, _ = h.shape
    z = jnp.einsum('btd,de->bte', h, w_qkv).astype(jnp.float32)
    q, k, v = [a.reshape(B, T, H_ATT, DH_ATT) for a in jnp.split(z, 3, axis=-1)]
    if k_cache is None:
        o = band_attention_prompt(q, k, v, rel_bias)
        keep = min(LEFT_CHUNKS * CHUNK, T)
        new_k, new_v = k[:, T - keep:], v[:, T - keep:]
    else:
        o = band_attention_sample(q, k, v, k_cache, v_cache, rel_bias)
        new_k, new_v = k, v
    out = jnp.einsum('bte,ed->btd', o.reshape(B, T, -1).astype(h.dtype), w_out)
    return out, new_k, new_v


def conv_ffn(h, conv_buf, w_up, conv_w, conv_b, w_down):
    B, T, _ = h.shape
    up = jnp.einsum('btd,df->btf', h, w_up)
    g, u = jnp.split(up, 2, axis=-1)
    if conv_buf is None:
        hist = jnp.zeros((B, CONV_W - 1, D_FF), g.dtype)
    else:
        hist = conv_buf.astype(g.dtype)
    gp = jnp.concatenate([hist, g], axis=1)
    gc = conv_b
    for w in range(CONV_W):
        gc = gc + gp[:, w:w + T] * conv_w[w]
    act = jax.nn.silu(gc) * u
    return jnp.einsum('btf,fd->btd', act, w_down), gp[:, -(CONV_W - 1):]


def trunk(x, pos, ret_in, gla_in, k_in, v_in, conv_in, p):
    ret_out, gla_out, k_out, v_out, conv_out = [], [], [], [], []
    for l in range(DEPTH):
        h = rmsnorm(x, p['norm_mix_g'][l])
        if l % 2 == 0:
            a = l // 2
            mix, s_r, s_g = ab_mixer(h, pos,
                                     None if ret_in is None else ret_in[a],
                                     None if gla_in is None else gla_in[a],
                                     p['w_in_ab'][a], p['gla_gate_w2'][a], p['gla_gate_b'][a],
                                     p['ret_norm_g'][a], p['gla_norm_g'][a], p['w_out_ab'][a])
            ret_out.append(s_r.astype(x.dtype))
            gla_out.append(s_g.astype(x.dtype))
        else:
            c = l // 2
            mix, nk, nv = c_mixer(h,
                                  None if k_in is None else k_in[c],
                                  None if v_in is None else v_in[c],
                                  p['w_qkv_att'][c], p['rel_bias_att'][c], p['w_out_att'][c])
            k_out.append(nk.astype(x.dtype))
            v_out.append(nv.astype(x.dtype))
        x = x + mix
        h = rmsnorm(x, p['norm_ffn_g'][l])
        f, buf = conv_ffn(h, None if conv_in is None else conv_in[l],
                          p['w_ffn_up'][l], p['ffn_conv_w'][l], p['ffn_conv_b'][l], p['w_ffn_down'][l])
        conv_out.append(buf.astype(x.dtype))
        x = x + f
    y = rmsnorm(x, p['norm_final_g'])
    return y, jnp.stack(ret_out), jnp.stack(gla_out), jnp.stack(k_out), jnp.stack(v_out), jnp.stack(conv_out)


def setup_inputs(seed: int = 0) -> dict:
    key = jax.random.key(seed)
    ks = jax.random.split(key, 24)
    nrm = jax.random.normal
    f32 = jnp.float32
    att_cache = min(LEFT_CHUNKS * CHUNK, PAST_LEN)
    return {
        'x_prompt': nrm(ks[0], (BATCH, SEQ, D_MODEL), f32),
        'x_sample': nrm(ks[1], (DEC_BATCH, DEC_SEQ, D_MODEL), f32),
        'state_ret': nrm(ks[2], (N_AB_LAYERS, DEC_BATCH, H_RET, DK_RET, DV_RET), f32),
        'state_gla': nrm(ks[3], (N_AB_LAYERS, DEC_BATCH, H_GLA, DK_GLA, DV_GLA), f32),
        'cache_attn_k': nrm(ks[4], (N_ATT_LAYERS, DEC_BATCH, att_cache, H_ATT, DH_ATT), f32),
        'cache_attn_v': nrm(ks[5], (N_ATT_LAYERS, DEC_BATCH, att_cache, H_ATT, DH_ATT), f32),
        'state_ffn_conv': nrm(ks[6], (DEPTH, DEC_BATCH, CONV_W - 1, D_FF), f32),
        'norm_mix_g': 1.0 + 0.05 * nrm(ks[7], (DEPTH, D_MODEL), f32),
        'w_in_ab': nrm(ks[8], (N_AB_LAYERS, D_MODEL, AB_IN_WIDTH), f32) * D_MODEL ** -0.5,
        'gla_gate_w2': nrm(ks[9], (N_AB_LAYERS, GLA_GATE_RANK, H_GLA * DK_GLA), f32) * GLA_GATE_RANK ** -0.5,
        'gla_gate_b': 0.1 * nrm(ks[10], (N_AB_LAYERS, H_GLA * DK_GLA), f32),
        'ret_norm_g': 1.0 + 0.05 * nrm(ks[11], (N_AB_LAYERS, DV_RET), f32),
        'gla_norm_g': 1.0 + 0.05 * nrm(ks[12], (N_AB_LAYERS, DV_GLA), f32),
        'w_out_ab': nrm(ks[13], (N_AB_LAYERS, MIX_WIDTH, D_MODEL), f32) * MIX_WIDTH ** -0.5,
        'w_qkv_att': nrm(ks[14], (N_ATT_LAYERS, D_MODEL, 3 * MIX_WIDTH), f32) * D_MODEL ** -0.5,
        'rel_bias_att': 0.2 * nrm(ks[15], (N_ATT_LAYERS, H_ATT, 2 * MAX_REL + 1), f32),
        'w_out_att': nrm(ks[16], (N_ATT_LAYERS, MIX_WIDTH, D_MODEL), f32) * MIX_WIDTH ** -0.5,
        'norm_ffn_g': 1.0 + 0.05 * nrm(ks[17], (DEPTH, D_MODEL), f32),
        'w_ffn_up': nrm(ks[18], (DEPTH, D_MODEL, 2 * D_FF), f32) * D_MODEL ** -0.5,
        'ffn_conv_w': nrm(ks[19], (DEPTH, CONV_W, D_FF), f32) * CONV_W ** -0.5,
        'ffn_conv_b': 0.02 * nrm(ks[20], (DEPTH, D_FF), f32),
        'w_ffn_down': nrm(ks[21], (DEPTH, D_FF, D_MODEL), f32) * D_FF ** -0.5,
        'norm_final_g': 1.0 + 0.05 * nrm(ks[22], (D_MODEL,), f32),
    }


def reference(x_prompt, x_sample, state_ret, state_gla, cache_attn_k, cache_attn_v, state_ffn_conv,
              norm_mix_g, w_in_ab, gla_gate_w2, gla_gate_b, ret_norm_g, gla_norm_g, w_out_ab,
              w_qkv_att, rel_bias_att, w_out_att,
              norm_ffn_g, w_ffn_up, ffn_conv_w, ffn_conv_b, w_ffn_down, norm_final_g):
    p = dict(norm_mix_g=norm_mix_g, w_in_ab=w_in_ab, gla_gate_w2=gla_gate_w2, gla_gate_b=gla_gate_b,
             ret_norm_g=ret_norm_g, gla_norm_g=gla_norm_g, w_out_ab=w_out_ab,
             w_qkv_att=w_qkv_att, rel_bias_att=rel_bias_att, w_out_att=w_out_att,
             norm_ffn_g=norm_ffn_g, w_ffn_up=w_ffn_up, ffn_conv_w=ffn_conv_w, ffn_conv_b=ffn_conv_b,
             w_ffn_down=w_ffn_down, norm_final_g=norm_final_g)
    pos_p = jnp.arange(x_prompt.shape[1])
    pos_s = PAST_LEN + jnp.arange(x_sample.shape[1])
    y_prompt, p_ret, p_gla, p_k, p_v, p_conv = trunk(x_prompt, pos_p, None, None, None, None, None, p)
    y_sample, s_ret, s_gla, s_k, s_v, s_conv = trunk(x_sample, pos_s, state_ret, state_gla,
                                                     cache_attn_k, cache_attn_v, state_ffn_conv, p)
    return (y_prompt, y_sample, p_ret, p_gla, p_k, p_v, p_conv, s_ret, s_gla, s_k, s_v, s_conv)
```

```python
import os
import numpy as np
from contextlib import ExitStack
import concourse.bass as bass
import concourse.mybir as mybir
from concourse.bass_utils import run_bass_kernel_spmd

F32 = mybir.dt.float32
BF = mybir.dt.bfloat16
AF = mybir.ActivationFunctionType
ALU = mybir.AluOpType
AX = mybir.AxisListType

D = 2048
DFF = 5632
NFB = DFF // 128
EPS = 1e-6
ENGINES = ("pe", "act", "dve", "pool", "sp")
SEM_LIMIT = 8000
NDSEM = 24
NEG = -1e30
LB = 639


class Dep:
    __slots__ = ("w", "r")

    def __init__(self):
        self.w = None
        self.r = []


class T:
    def __init__(self, ap, dep=None, deps=None):
        self.ap = ap
        if deps is not None:
            self.deps = list(deps)
        else:
            self.deps = [dep if dep is not None else Dep()]
        self.dep = self.deps[0]

    def __getitem__(self, idx):
        return self.ap[idx]


class Sched:
    def __init__(self):
        self.ops = {e: [] for e in ENGINES}
        self.tick = {e: 0 for e in ENGINES}
        self.dtick = {e: 0 for e in ENGINES}
        self.seen = {e: {} for e in ENGINES}
        self.semkeys = set()

    def _ckey(self, e, tick):
        ep = (tick - 1) // SEM_LIMIT
        k = (e, "c", ep)
        self.semkeys.add(k)
        return k, tick - ep * SEM_LIMIT

    def _dkey(self, e, dt):
        ep = (dt - 16) // SEM_LIMIT
        k = (e, "d", ep)
        self.semkeys.add(k)
        return k, dt - ep * SEM_LIMIT

    def _need(self, e, reads, writes, skip_self=False):
        need = {}

        def add(kv):
            if kv is None:
                return
            k, v = kv
            if skip_self and k[0] == e and k[1] == "c":
                return
            if need.get(k, 0) < v:
                need[k] = v

        for d in reads:
            for dp in d.deps:
                add(dp.w)
        for d in writes:
            for dp in d.deps:
                add(dp.w)
                for kv in dp.r:
                    add(kv)
        waits = []
        seen = self.seen[e]
        for k, v in need.items():
            if seen.get(k, 0) < v:
                seen[k] = v
                waits.append((k, v))
        return waits

    def _mark(self, kv, reads, writes):
        for d in reads:
            for dp in d.deps:
                dp.r.append(kv)
        for d in writes:
            for dp in d.deps:
                dp.w = kv
                dp.r = []

    def op(self, e, fn, reads=(), writes=()):
        waits = self._need(e, reads, writes, skip_self=(e == "pe"))
        self.tick[e] += 1
        kv = self._ckey(e, self.tick[e])
        self.ops[e].append((waits, fn, kv[0], 1))
        self._mark(kv, reads, writes)

    def dma(self, e, fn, reads=(), writes=()):
        waits = self._need(e, reads, writes)
        if not hasattr(self, "dslot"):
            self.dslot = {q: 0 for q in ENGINES}
            self.dcnt = {q: [0] * NDSEM for q in ENGINES}
        j = self.dslot[e]
        self.dslot[e] = (j + 1) % NDSEM
        k = (e, "d", j)
        self.semkeys.add(k)
        prev = self.dcnt[e][j]
        if prev and self.seen[e].get(k, 0) < prev:
            self.seen[e][k] = prev
            waits.append((k, prev))
        self.dcnt[e][j] = prev + 16
        self.dtick[e] += 16
        kv = (k, prev + 16)
        self.ops[e].append((waits, fn, k, 16))
        self._mark(kv, reads, writes)

    def _dcur(self):
        cur = []
        if hasattr(self, "dslot"):
            for q in ENGINES:
                for j in range(NDSEM):
                    if self.dcnt[q][j]:
                        cur.append(((q, "d", j), self.dcnt[q][j]))
        return cur

    def barrier(self):
        cur = []
        for e in ENGINES:
            if self.tick[e]:
                cur.append(self._ckey(e, self.tick[e]))
        cur += self._dcur()
        for e in ENGINES:
            waits = []
            for k, v in cur:
                if k[0] == e and k[1] == "c":
                    continue
                if self.seen[e].get(k, 0) < v:
                    self.seen[e][k] = v
                    waits.append((k, v))
            if waits:
                self.ops[e].append((waits, None, None, 0))

    def final_wait_all(self):
        cur = []
        for e in ENGINES:
            if self.tick[e] and e != "sp":
                cur.append(self._ckey(e, self.tick[e]))
        cur += self._dcur()
        self.ops["sp"].append((cur, None, None, 0))

    def emit(self, nc):
        keys = sorted(self.semkeys)
        with ExitStack() as es:
            sems = {}
            for k in keys:
                sems[k] = es.enter_context(nc.semaphore("s_%s_%s_%d" % k))
            block = es.enter_context(nc.Block())

            def run(eng, e):
                for waits, fn, k, inc in self.ops[e]:
                    for wk, wv in waits:
                        eng.wait_ge(sems[wk], wv)
                    if fn is not None:
                        fn(eng).then_inc(sems[k], inc)

            @block.tensor
            def _(eng):
                run(eng, "pe")

            @block.scalar
            def _(eng):
                run(eng, "act")

            @block.vector
            def _(eng):
                run(eng, "dve")

            @block.gpsimd
            def _(eng):
                run(eng, "pool")

            @block.sync
            def _(eng):
                run(eng, "sp")


class Pass:
    def __init__(self, kind, x0=0, l0_rest=(0, 1, 2, 3), l1_kv=(0, 1, 2, 3), l1_attn=(0, 1, 2, 3), l1_rest="full",
                 out=False, maskable=False, yrow=0, first=False):
        self.kind = kind
        self.x0 = x0
        self.l0_rest = list(l0_rest)
        self.l1_kv = list(l1_kv)
        self.l1_attn = list(l1_attn)
        self.l1_rest = l1_rest
        self.out = out
        self.maskable = maskable
        self.yrow = yrow
        self.first = first
        if kind == "p":
            self.tiles = [(i * 128, 128) for i in range(4)]
            self.T = 512
            self.C = 128
        else:
            self.tiles = [(i * 32, 32) for i in range(4)]
            self.T = 128
            self.C = 32
        self.xtiles = self.tiles if kind == "p" else [(0, 128)]

    def xsel(self, tsel):
        return list(tsel) if self.kind == "p" else [0]


def build_program(stop_after=None, npass=5):
    nc = bass.Bass("TRN2", target_bir_lowering=False)
    S = Sched()
    es = ExitStack()

    def din(name, shape):
        return nc.dram_tensor(name, list(shape), F32, kind="ExternalInput").ap()

    def dout(name, shape):
        return nc.dram_tensor(name, list(shape), F32, kind="ExternalOutput").ap()

    xp_d = din("xp", [2048, D])
    xs_d = din("xs", [128, D])
    st_ret_d = din("st_ret", [4, 4, 256, 256])
    st_gla_d = din("st_gla", [4, 4, 128, 256])
    ck_d = din("ck", [4, 512, D])
    cv_d = din("cv", [4, 512, D])
    st_conv_d = din("st_conv", [2, 4, 2, DFF])
    w_in_d = din("w_in_blk", [28, 128, 4096])
    w_lo_d = din("w_lo", [128, 256])
    w2_d = din("gate_w2", [16, 512])
    gateb_d = din("gate_b", [1, 512])
    retg_d = din("ret_g", [1, 256])
    glag_d = din("gla_g", [1, 256])
    w_oab_d = din("w_out_ab", [8, 128, 4096])
    w_qkv_d = din("w_qkv", [24, 128, 4096])
    rbr_d = din("rbr", [16, 513])
    w_oat_d = din("w_out_att", [8, 128, 4096])
    ng_d = din("norm_g", [5, 128, 16])
    gfin_d = din("norm_final_g", [1, D])
    w_up2_d = din("w_up2", [2, 4, 5, 2, 128, 4096])
    w_up1_d = din("w_up1", [2, 4, 2, 128, 2048])
    cw_d = din("conv_w", [2, 3, DFF])
    cb_d = din("conv_b", [2, DFF])
    w_dnA_d = din("w_dnA", [2, 4, 4, 128, 4096])
    w_dnB_d = din("w_dnB", [2, 4, 4, 128, 1536])
    ctab_d = din("ctab", [128, 385])
    rope_p_d = din("rope_p", [128, 2, 2048])
    rope_s_d = din("rope_s", [128, 2, 128])
    dtab_p_d = din("dtab_p", [128, 8, 512])
    dtab_s_d = din("dtab_s", [128, 8, 128])

    yp_d = dout("yp", [1024, D])
    ys_d = dout("ys", [128, D])
    p_ret_d = dout("p_ret", [4, 256, 256])
    p_gla_d = dout("p_gla", [4, 128, 256])
    p_k_d = dout("p_k", [512, D])
    p_v_d = dout("p_v", [512, D])
    p_conv_d = dout("p_conv", [2, 2, DFF])
    s_ret_d = dout("s_ret", [4, 4, 256, 256])
    s_gla_d = dout("s_gla", [4, 4, 128, 256])
    s_k_d = dout("s_k", [128, D])
    s_v_d = dout("s_v", [128, D])
    s_conv_d = dout("s_conv", [2, 4, 2, DFF])
    fsc_h = nc.dram_tensor("fscratch", [16, 64 * LB + 64], F32, kind="Internal")
    fsc_d = T(fsc_h.ap())

    def sb(name, shape, dt):
        return T(es.enter_context(nc.sbuf_tensor("sb_" + name, list(shape), dt)))

    def psum(name, shape, dt):
        return T(es.enter_context(nc.psum_tensor("ps_" + name, list(shape), dt)))

    pfs = [psum("pf%d" % i, [128, 512], F32) for i in range(4)]
    psc = psum("psc", [128, 1024], F32)
    psc = T(psc.ap, deps=[Dep(), Dep()])
    pfs.append(T(psc.ap[:, 0:512], psc.deps[0]))
    pfs.append(T(psc.ap[:, 512:1024], psc.deps[1]))
    pbs = [psum("pb%d" % i, [128, 8, 128], BF) for i in range(2)]
    cnt = {"pf": 0, "pb": 0, "w": 0}

    def PF():
        cnt["pf"] += 1
        return pfs[cnt["pf"] % 6]

    def PB():
        cnt["pb"] += 1
        return pbs[cnt["pb"] % 2]

    ctab = sb("ctab", [128, 385], F32)
    maskT = ctab.ap[:, 0:128]
    tri = ctab.ap[:, 128:256]
    maskval = ctab.ap[:, 384:385]
    identb = sb("identb", [128, 128], BF)
    xt = [sb("x%d" % i, [128, D], F32) for i in range(4)]
    hT = sb("hT", [128, 16, 512], BF)
    NW = 5
    wbs = [sb("w%d" % i, [128, 4096], BF) for i in range(NW)]
    st = sb("st", [128, 8], F32)
    ngc = sb("ngc", [128, 5, 16], F32)
    rope = sb("rope", [128, 2, 512], F32)
    dtab = sb("dtab", [128, 2, 512], F32)
    Sf_ret = [sb("sfr%d" % h, [128, 512], F32) for h in range(4)]
    Sf_gla = [sb("sfg%d" % h, [128, 256], F32) for h in range(4)]
    Sb = sb("Sb", [128, 512], BF)
    ghist = [sb("ghist%d" % l, [128, NFB, 2], F32) for l in range(2)]
    cw = [sb("cw%d" % l, [128, 3, NFB], F32) for l in range(2)]
    cbt = [sb("cb%d" % l, [128, NFB], F32) for l in range(2)]
    khist = sb("khist", [128, 16, 512], BF)
    vhist = sb("vhist", [128, 4, D], BF)
    oo = sb("oo", [128, 4 * D], BF)
    o_out = [T(oo.ap[:, i * D:(i + 1) * D]) for i in range(4)]
    actT = T(oo.ap[:, 0:11 * 512].rearrange("p (f t) -> p f t", t=512))
    hb = T(oo.ap[:, 0:D], o_out[0].dep)
    tmpall = sb("tmpall", [128, 2048], F32)
    tmp = [T(tmpall.ap[:, i * 512:(i + 1) * 512]) for i in range(4)]
    qh = sb("qh", [128, 2, 512], BF)
    kh = sb("kh", [128, 2, 512], BF)
    kst = sb("kst", [128, 512], BF)
    v_sb = [sb("v%d" % i, [128, 256], BF) for i in range(4)]
    g_sb = [sb("g%d" % i, [128, 256], F32) for i in range(4)]
    scT = sb("scT", [128, 128], BF)
    kT_sb = sb("kT_sb", [128, 2, 128], BF)
    ogt = sb("ogt", [128, 256], F32)
    ngb = [sb("ngb%d" % i, [128, 256], F32) for i in range(2)]
    lo_sb = sb("lo_sb", [16, 512], F32)
    w2_sb = sb("w2_sb", [16, 512], F32)
    gateb = sb("gateb", [128, 512], F32)
    la = [sb("la%d" % i, [128, 512], F32) for i in range(4)]
    Eq = sb("Eq", [128, 512], F32)
    Ek = sb("Ek", [128, 512], F32)
    gext = T(rope.ap[:, :, :].rearrange("p a t -> p (a t)")[:, 0:520], rope.dep)
    acc = T(dtab.ap[:, 0, :], dtab.dep)
    ghs_in = [[sb("ghsi%d_%d" % (l, s), [128, NFB, 2], F32) for s in range(4)] for l in range(2)]
    ghs_out = ghs_in
    qT = qh
    kcur = v_sb
    vcur = [T(g_sb[i].ap[:, :].bitcast(BF)[:, 0:256], g_sb[i].dep) for i in range(4)]
    kTcur = kh
    kvst = [ogt] * 2
    bias2 = [T(tmpall.ap[:, 0:640], deps=[tmp[0].dep, tmp[1].dep]),
             T(tmpall.ap[:, 640:1280], deps=[tmp[1].dep, tmp[2].dep])]
    s_sb = T(tmpall.ap[:, 1280:1920], deps=[tmp[2].dep, tmp[3].dep])
    p_bf = T(Eq.ap[:, :].bitcast(BF)[:, 0:640], Eq.dep)
    p_bfs = [p_bf, T(la[3].ap[:, :].bitcast(BF)[:, 0:640], la[3].dep)]
    sts = [sb("stA", [128, 8], F32), sb("stB", [128, 8], F32)]
    pT_sb = T(Ek.ap[:, :].bitcast(BF)[:, 0:640].rearrange("p (a c) -> p a c", c=128), Ek.dep)
    kc_tm = T(la[0].ap[:, :].bitcast(BF).rearrange("p (a c) -> p a c", c=256), la[0].dep)
    vc_tm = T(la[1].ap[:, :].bitcast(BF).rearrange("p (a c) -> p a c", c=256), la[1].dep)
    kTc = T(la[2].ap[:, :].bitcast(BF).rearrange("p (a c) -> p a c", c=128), la[2].dep)
    erb = T(s_sb.ap[0:16, 0:LB], deps=s_sb.deps)

    def mm(out, lhsT, rhs, start, stop, reads, writes):
        S.op("pe", lambda e: e.matmul(out, lhsT=lhsT, rhs=rhs, start=start, stop=stop), reads, writes)

    def tr(out, in_, ident, reads, writes):
        S.op("pe", lambda e: e.transpose(out=out, in_=in_, identity=ident), list(reads) + [identb], writes)

    def act(out, in_, func, reads, writes, **kw):
        S.op("act", lambda e: e.activation(out=out, in_=in_, func=func, **kw), reads, writes)

    def tt(out, in0, in1, op, reads, writes, eng="dve"):
        S.op(eng, lambda e: e.tensor_tensor(out=out, in0=in0, in1=in1, op=op), reads, writes)

    def ts(out, in0, s1, s2, op0, op1, reads, writes, eng="dve"):
        S.op(eng, lambda e: e.tensor_scalar(out=out, in0=in0, scalar1=s1, scalar2=s2, op0=op0, op1=op1), reads, writes)

    def stt(out, in0, scalar, in1, op0, op1, reads, writes, eng="dve"):
        S.op(eng, lambda e: e.scalar_tensor_tensor(out=out, in0=in0, scalar=scalar, in1=in1, op0=op0, op1=op1), reads, writes)

    def cp(out, in_, reads, writes, eng="dve"):
        S.op(eng, lambda e: e.tensor_copy(out=out, in_=in_), reads, writes)

    def dma(q, out, in_, reads, writes, slow=False):
        if slow:
            S.dma(q, lambda e: e.dma_start(out=out, in_=in_, allow_slow_non_contiguous=True), reads, writes)
        else:
            S.dma(q, lambda e: e.dma_start(out=out, in_=in_), reads, writes)

    def load_w(src2, nel):
        cnt["w"] += 1
        w = wbs[cnt["w"] % NW]
        dma("pool", w.ap[:, 0:nel], src2, [], [w])
        return w

    def rstd_from(col_ss, n, inv_n):
        ts(st.ap[0:n, 1:2], col_ss, inv_n, EPS, ALU.mult, ALU.add, [st], [st])
        act(st.ap[0:n, 2:3], st.ap[0:n, 1:2], AF.Sqrt, [st], [st])
        S.op("dve", lambda e: e.reciprocal(out=st.ap[0:n, 3:4], in_=st.ap[0:n, 2:3]), [st], [st])

    def norm_hT(ps, gi, tsel=(0, 1, 2, 3)):
        for t in ps.xsel(tsel):
            off, n = ps.xtiles[t]
            act(hb.ap[0:n, :], xt[t].ap[0:n, :], AF.Square, [xt[t]], [hb, st], accum_out=st.ap[0:n, 0:1])
            rstd_from(st.ap[0:n, 0:1], n, 1.0 / D)
            ts(hb.ap[0:n, :], xt[t].ap[0:n, :], st.ap[0:n, 3:4], 1.0, ALU.mult, ALU.mult, [xt[t], st], [hb])
            for g8 in range(2):
                pb = PB()
                for j in range(8):
                    kc = g8 * 8 + j
                    tr(pb.ap[:, j, 0:n], hb.ap[0:n, kc * 128:(kc + 1) * 128], identb.ap[0:n, 0:n], [hb], [pb])
                for j in range(8):
                    kc = g8 * 8 + j
                    eng = "dve" if j % 2 == 0 else "pool"
                    if eng == "pool":
                        act(hT.ap[:, kc, off:off + n], pb.ap[:, j, 0:n], AF.Copy, [pb, ngc], [hT], scale=ngc.ap[:, gi, kc:kc + 1])
                    else:
                        ts(hT.ap[:, kc, off:off + n], pb.ap[:, j, 0:n], ngc.ap[:, gi, kc:kc + 1], 1.0, ALU.mult, ALU.mult, [pb, ngc], [hT])

    def o_to_hT(ps, tsel=(0, 1, 2, 3)):
        for t in tsel:
            off, n = ps.tiles[t]
            for g8 in range(2):
                pb = PB()
                for j in range(8):
                    kc = g8 * 8 + j
                    tr(pb.ap[:, j, 0:n], o_out[t].ap[0:n, kc * 128:(kc + 1) * 128], identb.ap[0:n, 0:n], [o_out[t]], [pb])
                if g8 == 0:
                    act(hT.ap[:, 0:8, off:off + n], pb.ap[:, :, 0:n], AF.Copy, [pb], [hT])
                else:
                    cp(hT.ap[:, 8:16, off:off + n], pb.ap[:, :, 0:n], [pb], [hT])

    def out_proj(ps, w_d, tsel=(0, 1, 2, 3)):
        for cb in range(8):
            w = load_w(w_d[cb], 4096)
            for t in ps.xsel(tsel):
                off, n = ps.xtiles[t]
                p = PF()
                for kc in range(16):
                    mm(p.ap[0:n, 0:256], hT.ap[:, kc, off:off + n], w.ap[:, kc * 256:(kc + 1) * 256], kc == 0, kc == 15, [hT, w], [p])
                xs_ = xt[t].ap[0:n, cb * 256:(cb + 1) * 256]
                tt(xs_, xs_, p.ap[0:n, 0:256], ALU.add, [xt[t], p], [xt[t]])

    def vg_proj(ps, cv0, cg0):
        wv = load_w(w_in_d[cv0 // 256], 4096)
        wg = load_w(w_in_d[cg0 // 256], 4096)
        for t, (off, n) in enumerate(ps.tiles):
            p = PF()
            for kc in range(16):
                mm(p.ap[0:n, 0:256], hT.ap[:, kc, off:off + n], wv.ap[:, kc * 256:(kc + 1) * 256], kc == 0, kc == 15, [hT, wv], [p])
            for kc in range(16):
                mm(p.ap[0:n, 256:512], hT.ap[:, kc, off:off + n], wg.ap[:, kc * 256:(kc + 1) * 256], kc == 0, kc == 15, [hT, wg], [p])
            act(v_sb[t].ap[0:n, :], p.ap[0:n, 0:256], AF.Copy, [p], [v_sb[t]])
            act(g_sb[t].ap[0:n, :], p.ap[0:n, 256:512], AF.Silu, [p], [g_sb[t]])

    def recur(ps, ndc, Sf, st_in, st_out, p_out, kscale, ngt, ocol0, gamma_c=None, elast=False):
        C = ps.C
        for t, (off, n) in enumerate(ps.tiles):
            W_ = ndc * 256
            if ps.kind == "s":
                dma("sp", Sf.ap[:, 0:W_].rearrange("p (c v) -> p c v", v=256),
                    st_in[t].rearrange("(c p) v -> p c v", p=128), [], [Sf])
            act(Sb.ap[:, 0:W_], Sf.ap[:, 0:W_], AF.Copy, [Sf], [Sb])
            pS = PF()
            for dc in range(ndc):
                mm(pS.ap[0:n, 0:n], kh.ap[:, dc, off:off + n], qh.ap[:, dc, off:off + n], dc == 0, dc == ndc - 1, [kh, qh], [pS])
            pb = PB()
            ksrc = kst if elast else kh
            for dc in range(ndc):
                src = ksrc.ap[:, off:off + n] if elast else ksrc.ap[:, dc, off:off + n]
                tr(pb.ap[0:n, dc, :], src, identb.ap[:, :], [ksrc], [pb])
            tt(scT.ap[0:n, 0:n], pS.ap[0:n, 0:n], maskT[0:n, 0:n], ALU.mult, [pS, ctab], [scT])
            act(kT_sb.ap[0:n, 0:ndc, :], pb.ap[0:n, 0:ndc, :], AF.Copy, [pb], [kT_sb], scale=float(kscale))
            pO = PF()
            mm(pO.ap[0:n, 0:256], scT.ap[0:n, 0:n], v_sb[t].ap[0:n, :], True, False, [scT, v_sb[t]], [pO])
            for dc in range(ndc):
                mm(pO.ap[0:n, 0:256], qh.ap[:, dc, off:off + n], Sb.ap[:, dc * 256:(dc + 1) * 256], False, dc == ndc - 1, [qh, Sb], [pO])
            pD = PF()
            for dc in range(ndc):
                mm(pD.ap[:, dc * 256:(dc + 1) * 256], kT_sb.ap[0:n, dc, :], v_sb[t].ap[0:n, :], True, True, [kT_sb, v_sb[t]], [pD])
            if elast:
                sc_ = Eq.ap[:, off + n - 1:off + n]
                stt(Sf.ap[:, 0:W_], Sf.ap[:, 0:W_], sc_, pD.ap[:, 0:W_], ALU.mult, ALU.add, [Sf, Eq, pD], [Sf])
            else:
                stt(Sf.ap[:, 0:W_], Sf.ap[:, 0:W_], float(gamma_c), pD.ap[:, 0:W_], ALU.mult, ALU.add, [Sf, pD], [Sf])
            act(ogt.ap[0:n, :], pO.ap[0:n, 0:256], AF.Square, [pO], [ogt, st], accum_out=st.ap[0:n, 0:1])
            rstd_from(st.ap[0:n, 0:1], n, 1.0 / 256)
            stt(ogt.ap[0:n, :], pO.ap[0:n, 0:256], st.ap[0:n, 3:4], ngt.ap[0:n, :], ALU.mult, ALU.mult, [pO, st, ngt], [ogt])
            tt(o_out[t].ap[0:n, ocol0:ocol0 + 256], ogt.ap[0:n, :], g_sb[t].ap[0:n, :], ALU.mult, [ogt, g_sb[t]], [o_out[t]])
            if ps.kind == "s":
                dma("sp", st_out[t].rearrange("(c p) v -> p c v", p=128),
                    Sf.ap[:, 0:W_].rearrange("p (c v) -> p c v", v=256), [Sf], [])
        if ps.kind == "p" and ps.out:
            dma("sp", p_out.rearrange("(c p) v -> p c v", p=128),
                Sf.ap[:, 0:ndc * 256].rearrange("p (c v) -> p c v", v=256), [Sf], [])

    def l0_mixer(ps):
        T_ = ps.T
        dtab_d = dtab_p_d if ps.kind == "p" else dtab_s_d
        gam = [1.0 - 2.0 ** (-5 - h) for h in range(4)]
        cos_ = rope.ap[:, 0, 0:T_]
        sin_ = rope.ap[:, 1, 0:T_]
        for h in range(4):
            dma("sp", dtab.ap[:, :, 0:T_], dtab_d[:, 2 * h:2 * h + 2, 0:T_], [], [dtab])
            for di, c0, dst in ((0, h * 256, qh), (1, 1024 + h * 256, kh)):
                w = load_w(w_in_d[c0 // 256], 4096)
                p1 = PF()
                p2 = PF()
                for half, pp in ((0, p1), (1, p2)):
                    for kc in range(16):
                        mm(pp.ap[:, 0:T_], w.ap[:, kc * 256 + half * 128:kc * 256 + half * 128 + 128], hT.ap[:, kc, 0:T_], kc == 0, kc == 15, [hT, w], [pp])
                dec = dtab.ap[:, di, 0:T_]
                tA, tB, tC, tD = [x.ap[:, 0:T_] for x in tmp]
                tt(tA, p1.ap[:, 0:T_], dec, ALU.mult, [p1, dtab], [tmp[0]])
                tt(tB, p2.ap[:, 0:T_], dec, ALU.mult, [p2, dtab], [tmp[1]])
                tt(tC, tA, cos_, ALU.mult, [tmp[0], rope], [tmp[2]])
                tt(tD, tB, sin_, ALU.mult, [tmp[1], rope], [tmp[3]])
                tt(dst.ap[:, 0, 0:T_], tC, tD, ALU.subtract, [tmp[2], tmp[3]], [dst])
                tt(tC, tA, sin_, ALU.mult, [tmp[0], rope], [tmp[2]])
                tt(tD, tB, cos_, ALU.mult, [tmp[1], rope], [tmp[3]])
                tt(dst.ap[:, 1, 0:T_], tC, tD, ALU.add, [tmp[2], tmp[3]], [dst])
            vg_proj(ps, 2048 + h * 256, 3072 + h * 256)
            gC = gam[h] ** ps.C
            recur(ps, 2, Sf_ret[h],
                  [st_ret_d[s, h] for s in range(4)], [s_ret_d[s, h] for s in range(4)], p_ret_d[h],
                  gC, ngb[0], h * 256, gamma_c=gC)
        wlo = load_w(w_lo_d[:, :], 256)
        p = PF()
        for kc in range(16):
            mm(p.ap[0:16, 0:T_], wlo.ap[:, kc * 16:(kc + 1) * 16], hT.ap[:, kc, 0:T_], kc == 0, kc == 15, [hT, wlo], [p])
        act(lo_sb.ap[0:16, 0:T_], p.ap[0:16, 0:T_], AF.Copy, [p], [lo_sb])
        for t, (off, n) in enumerate(ps.tiles):
            p = PF()
            mm(p.ap[0:n, 0:512], lo_sb.ap[0:16, off:off + n], w2_sb.ap[0:16, :], True, True, [lo_sb, w2_sb], [p])
            tA, tB, tC = tmp[0].ap[0:n, :], tmp[1].ap[0:n, :], tmp[2].ap[0:n, :]
            tt(tA, p.ap[0:n, 0:512], gateb.ap[0:n, :], ALU.add, [p, gateb], [tmp[0]])
            act(tB, tA, AF.Abs, [tmp[0]], [tmp[1]])
            tt(tC, tA, tB, ALU.subtract, [tmp[0], tmp[1]], [tmp[2]])
            act(tB, tB, AF.Exp, [tmp[1]], [tmp[1]], scale=-1.0)
            act(tB, tB, AF.Ln, [tmp[1]], [tmp[1]], bias=1.0)
            stt(la[t].ap[0:n, :], tC, 0.5, tB, ALU.mult, ALU.subtract, [tmp[2], tmp[1]], [la[t]])
        for h in range(4):
            for t, (off, n) in enumerate(ps.tiles):
                p = PF()
                mm(p.ap[:, 0:n], la[t].ap[0:n, h * 128:(h + 1) * 128], tri[0:n, 0:n], True, True, [la[t], ctab], [p])
                act(Eq.ap[:, off:off + n], p.ap[:, 0:n], AF.Exp, [p], [Eq])
                act(Ek.ap[:, off:off + n], p.ap[:, 0:n], AF.Exp, [p], [Ek], scale=-1.0)
            cnt["w"] += 1
            w = wbs[cnt["w"] % NW]
            wv3 = w.ap[:, 0:4096].rearrange("p (k n) -> p k n", n=256)
            hoff = (h % 2) * 128
            dma("pool", wv3[:, :, 0:128], w_in_d[16 + h // 2].rearrange("p (k n) -> p k n", n=256)[:, :, hoff:hoff + 128], [], [w])
            dma("pool", wv3[:, :, 128:256], w_in_d[18 + h // 2].rearrange("p (k n) -> p k n", n=256)[:, :, hoff:hoff + 128], [], [w])
            pq = PF()
            pk = PF()
            for half, pp in ((0, pq), (1, pk)):
                for kc in range(16):
                    mm(pp.ap[:, 0:T_], w.ap[:, kc * 256 + half * 128:kc * 256 + half * 128 + 128], hT.ap[:, kc, 0:T_], kc == 0, kc == 15, [hT, w], [pp])
            stt(qh.ap[:, 0, 0:T_], pq.ap[:, 0:T_], float(128 ** -0.5), Eq.ap[:, 0:T_], ALU.mult, ALU.mult, [pq, Eq], [qh])
            tt(kh.ap[:, 0, 0:T_], pk.ap[:, 0:T_], Ek.ap[:, 0:T_], ALU.mult, [pk, Ek], [kh])
            for t, (off, n) in enumerate(ps.tiles):
                stt(kst.ap[:, off:off + n], pk.ap[:, off:off + n], Eq.ap[:, off + n - 1:off + n], Ek.ap[:, off:off + n],
                    ALU.mult, ALU.mult, [pk, Eq, Ek], [kst])
            vg_proj(ps, 5120 + h * 256, 6144 + h * 256)
            recur(ps, 1, Sf_gla[h],
                  [st_gla_d[s, h] for s in range(4)], [s_gla_d[s, h] for s in range(4)], p_gla_d[h],
                  1.0, ngb[1], 1024 + h * 256, elast=True)

    def ffn(ps, l, tsel=(0, 1, 2, 3), hist_only=False):
        if ps.kind == "p":
            tok0 = ps.tiles[tsel[0]][0]
            Lt = sum(ps.tiles[t][1] for t in tsel)
            segs = [(tok0, Lt, None)]
        else:
            tok0, Lt = 0, ps.T
            segs = [(off, n, s) for s, (off, n) in enumerate(ps.tiles)]
        S.barrier()
        for q in range(4):
            f0 = q * 11
            units = [(f0 + 2 * i, 2) for i in range(5)] + [(f0 + 10, 1)]
            for ui, (fb0, nsub) in enumerate(units):
                c0 = fb0 * 128
                ncl = nsub * 128
                srcg = w_up2_d[l, q, ui, 0] if nsub == 2 else w_up1_d[l, q, 0]
                srcu = w_up2_d[l, q, ui, 1] if nsub == 2 else w_up1_d[l, q, 1]
                wg = load_w(srcg, 16 * ncl)
                if not hist_only:
                    wu = load_w(srcu, 16 * ncl)
                for sub in range(nsub):
                    fb = fb0 + sub
                    fbl = fb - f0
                    pg = PF()
                    for kc in range(16):
                        mm(pg.ap[:, tok0:tok0 + Lt], wg.ap[:, kc * ncl + sub * 128:kc * ncl + sub * 128 + 128], hT.ap[:, kc, tok0:tok0 + Lt], kc == 0, kc == 15, [hT, wg], [pg])
                    if hist_only:
                        ts(ghist[l].ap[:, fb, :], pg.ap[:, tok0 + Lt - 2:tok0 + Lt], 1.0, 0.0, ALU.mult, ALU.add, [pg], [ghist[l]])
                        continue
                    pu = PF()
                    for kc in range(16):
                        mm(pu.ap[:, tok0:tok0 + Lt], wu.ap[:, kc * ncl + sub * 128:kc * ncl + sub * 128 + 128], hT.ap[:, kc, tok0:tok0 + Lt], kc == 0, kc == 15, [hT, wu], [pu])
                    for (off, L, s) in segs:
                        hsrc = ghist[l] if s is None else ghs_in[l][s]
                        hdst = ghist[l] if s is None else ghs_out[l][s]
                        act(gext.ap[:, 0:2], hsrc.ap[:, fb, :], AF.Copy, [hsrc], [gext])
                        act(gext.ap[:, 2:2 + L], pg.ap[:, off:off + L], AF.Copy, [pg], [gext])
                        ts(acc.ap[:, 0:L], gext.ap[:, 0:L], cw[l].ap[:, 0, fb:fb + 1], cbt[l].ap[:, fb:fb + 1], ALU.mult, ALU.add, [gext, cw[l], cbt[l]], [acc])
                        stt(acc.ap[:, 0:L], gext.ap[:, 1:L + 1], cw[l].ap[:, 1, fb:fb + 1], acc.ap[:, 0:L], ALU.mult, ALU.add, [gext, cw[l], acc], [acc])
                        stt(acc.ap[:, 0:L], gext.ap[:, 2:L + 2], cw[l].ap[:, 2, fb:fb + 1], acc.ap[:, 0:L], ALU.mult, ALU.add, [gext, cw[l], acc], [acc])
                        act(acc.ap[:, 0:L], acc.ap[:, 0:L], AF.Silu, [acc], [acc])
                        tt(actT.ap[:, fbl, off:off + L], acc.ap[:, 0:L], pu.ap[:, off:off + L], ALU.mult, [acc, pu], [actT])
                        act(hdst.ap[:, fb, :], gext.ap[:, L:L + 2], AF.Copy, [gext], [hdst])
            if hist_only:
                continue
            for cb in range(4):
                for (kb0, nr) in ((0, 8), (8, 3)):
                    w = load_w((w_dnA_d if kb0 == 0 else w_dnB_d)[l, q, cb], nr * 512)
                    for t in ps.xsel(tsel):
                        off, n = ps.xtiles[t]
                        for s_ in range(nr):
                            mm(pfs[t].ap[0:n, 0:512], actT.ap[:, kb0 + s_, off:off + n], w.ap[:, s_ * 512:(s_ + 1) * 512],
                               kb0 == 0 and s_ == 0, kb0 == 8 and s_ == nr - 1, [actT, w], [pfs[t]])
                for t in ps.xsel(tsel):
                    off, n = ps.xtiles[t]
                    xs_ = xt[t].ap[0:n, cb * 512:(cb + 1) * 512]
                    tt(xs_, xs_, pfs[t].ap[0:n, 0:512], ALU.add, [xt[t], pfs[t]], [xt[t]])
        S.barrier()
        if ps.kind == "p" and ps.out:
            for r in range(2):
                dma("sp", p_conv_d[l, r].rearrange("(fb p) -> p fb", p=128), ghist[l].ap[:, :, r], [ghist[l]], [], slow=True)
        if ps.kind == "s":
            for s in range(4):
                for r in range(2):
                    dma("sp", s_conv_d[l, s, r].rearrange("(fb p) -> p fb", p=128), ghs_out[l][s].ap[:, :, r], [ghs_out[l][s]], [], slow=True)

    def l1_attn(ps):
        T_ = ps.T
        kv_t = ps.l1_kv
        at_t = ps.l1_attn
        full = len(at_t) > 0
        if full:
            for i in range(2):
                S.op("dve", lambda e, i=i: e.memset(bias2[i].ap[:, :], NEG), [], [bias2[i]])
        for hp in range(8):
            if full:
                q0 = ps.tiles[at_t[0]][0]
                q1 = ps.tiles[at_t[-1]][0] + ps.tiles[at_t[-1]][1]
                wq = load_w(w_qkv_d[hp], 4096)
                for hh in range(2):
                    p = PF()
                    for kc in range(16):
                        mm(p.ap[:, q0:q1], wq.ap[:, kc * 256 + hh * 128:kc * 256 + hh * 128 + 128], hT.ap[:, kc, q0:q1], kc == 0, kc == 15, [hT, wq], [p])
                    act(qT.ap[:, hh, q0:q1], p.ap[:, q0:q1], AF.Copy, [p], [qT], scale=float(128 ** -0.5))
            wk = load_w(w_qkv_d[8 + hp], 4096)
            wv = load_w(w_qkv_d[16 + hp], 4096)
            want_out = ((ps.kind == "s") or ps.out) and not os.environ.get("DBG_NOOUT")
            for which, w, cur, od in (("k", wk, kcur, (s_k_d if ps.kind == "s" else p_k_d)),
                                      ("v", wv, vcur, (s_v_d if ps.kind == "s" else p_v_d))):
                for t in kv_t:
                    off, n = ps.tiles[t]
                    p = PF()
                    for kc in range(16):
                        mm(p.ap[0:n, 0:256], hT.ap[:, kc, off:off + n], w.ap[:, kc * 256:(kc + 1) * 256], kc == 0, kc == 15, [hT, w], [p])
                    act(cur[t].ap[0:n, :], p.ap[0:n, 0:256], AF.Copy, [p], [cur[t]])
                    if want_out:
                        stg = kvst[(t + (which == "v")) % 2]
                        if os.environ.get("DBG_STG"):
                            stg = ngb[0]
                        if not os.environ.get("DBG_NOCP"):
                            act(stg.ap[0:n, :], p.ap[0:n, 0:256], AF.Copy, [p], [stg])
                        if not os.environ.get("DBG_NODMA"):
                            dma(os.environ.get("DBG_OQ", "sp"), od[off:off + n, hp * 256:(hp + 1) * 256], stg.ap[0:n, :], [stg], [])
                    if which == "k" and not os.environ.get("DBG_NOTR"):
                        pb = PB()
                        for hh in range(2):
                            tr(pb.ap[:, hh, 0:n], cur[t].ap[0:n, hh * 128:(hh + 1) * 128], identb.ap[0:n, 0:n], [cur[t]], [pb])
                        act(kTcur.ap[:, :, off:off + n], pb.ap[:, 0:2, 0:n], AF.Copy, [pb], [kTcur])
            if full:
                for hh in range(2):
                    h = hp * 2 + hh
                    b2 = bias2[hh]
                    base = h * (64 * LB + 64) + 63
                    srcA = bass.AP(tensor=fsc_h, offset=base, ap=[[LB - 1, 64], [1, 576]])
                    dma("sp", b2.ap[0:64, 0:576], srcA, [fsc_d], [b2])
                    dma("sp", b2.ap[64:128, 64:640], srcA, [fsc_d], [b2])
                if ps.kind == "p":
                    order = [(t, hh) for hh in range(2) for t in at_t]
                else:
                    order = [(t, hh) for t in at_t for hh in range(2)]
                ctxs = {}

                def stageA(idx, t, hh):
                    off, n = ps.tiles[t]
                    h = hp * 2 + hh
                    b2 = bias2[hh]
                    pbuf = p_bfs[idx % 2]
                    stt_ = sts[idx % 2]
                    nmask = 0
                    if ps.kind == "p":
                        blocks = []
                        for kb in range(5):
                            tt_ = t - 4 + kb
                            if ps.x0 + tt_ * 128 < 1024:
                                nmask = (kb + 1) * 128
                            if tt_ < 0:
                                blocks.append((khist.ap[:, h, (4 + tt_) * 128:(5 + tt_) * 128], [khist],
                                               vhist.ap[:, 4 + tt_, h * 128:(h + 1) * 128], [vhist], 128))
                            else:
                                blocks.append((kTcur.ap[:, hh, tt_ * 128:(tt_ + 1) * 128], [kTcur],
                                               vcur[tt_].ap[:, hh * 128:(hh + 1) * 128], [vcur[tt_]], 128))
                        nk_tot = 640
                    else:
                        if hh == 0:
                            dma("pool", kc_tm.ap[:, :, :], ck_d[t, :, hp * 256:(hp + 1) * 256].rearrange("(a p) c -> p a c", p=128), [], [kc_tm])
                            dma("pool", vc_tm.ap[:, :, :], cv_d[t, :, hp * 256:(hp + 1) * 256].rearrange("(a p) c -> p a c", p=128), [], [vc_tm])
                            pb = PB()
                            for h2 in range(2):
                                for a in range(4):
                                    tr(pb.ap[:, h2 * 4 + a, :], kc_tm.ap[:, a, h2 * 128:(h2 + 1) * 128], identb.ap[:, :], [kc_tm], [pb])
                            act(kTc.ap[:, :, :], pb.ap[:, :, :], AF.Copy, [pb], [kTc])
                        blocks = []
                        for kb in range(4):
                            blocks.append((kTc.ap[:, hh * 4 + kb, :], [kTc], vc_tm.ap[:, kb, hh * 128:(hh + 1) * 128], [vc_tm], 128))
                        blocks.append((kTcur.ap[:, hh, off:off + n], [kTcur], vcur[t].ap[0:n, hh * 128:(hh + 1) * 128], [vcur[t]], n))
                        nk_tot = 512 + n
                    c = 0
                    for (kap, kdeps, vap, vdeps, nk) in blocks:
                        mm(psc.ap[0:n, c:c + nk], qT.ap[:, hh, off:off + n], kap, True, True, [qT] + kdeps, [psc])
                        c += nk
                    tt(s_sb.ap[0:n, 0:512], psc.ap[0:n, 0:512], b2.ap[0:n, 0:512], ALU.add, [psc, b2], [s_sb])
                    tt(s_sb.ap[0:n, 512:nk_tot], psc.ap[0:n, 512:nk_tot], b2.ap[0:n, 512:nk_tot], ALU.add, [psc, b2], [s_sb])
                    if nmask:
                        ts(s_sb.ap[0:n, 0:nmask], s_sb.ap[0:n, 0:nmask], maskval[0:n, :], 0.0, ALU.add, ALU.add, [s_sb, ctab], [s_sb])
                    S.op("dve", lambda e, n=n, nk_tot=nk_tot: e.reduce_max(out=stt_.ap[0:n, 4:5], in_=s_sb.ap[0:n, 0:nk_tot], axis=AX.X), [s_sb], [stt_])
                    ts(stt_.ap[0:n, 5:6], stt_.ap[0:n, 4:5], -1.0, 0.0, ALU.mult, ALU.add, [stt_], [stt_])
                    act(pbuf.ap[0:n, 0:nk_tot], s_sb.ap[0:n, 0:nk_tot], AF.Exp, [s_sb, stt_], [pbuf, stt_], bias=stt_.ap[0:n, 5:6], accum_out=stt_.ap[0:n, 6:7])
                    ctxs[idx] = (t, hh, off, n, h, blocks, pbuf, stt_)

                def stageB(idx):
                    (t, hh, off, n, h, blocks, pbuf, stt_) = ctxs.pop(idx)
                    pb = PB()
                    c = 0
                    for kb, (kap, kdeps, vap, vdeps, nk) in enumerate(blocks):
                        tr(pb.ap[0:nk, kb, 0:n], pbuf.ap[0:n, c:c + nk], identb.ap[0:n, 0:n], [pbuf], [pb])
                        c += nk
                    nlast = blocks[-1][4]
                    if nlast == 128:
                        act(pT_sb.ap[:, 0:5, 0:n], pb.ap[:, 0:5, 0:n], AF.Copy, [pb], [pT_sb])
                    else:
                        act(pT_sb.ap[:, 0:4, 0:n], pb.ap[:, 0:4, 0:n], AF.Copy, [pb], [pT_sb])
                        act(pT_sb.ap[0:nlast, 4, 0:n], pb.ap[0:nlast, 4, 0:n], AF.Copy, [pb], [pT_sb])
                    po = PF()
                    for kb, (kap, kdeps, vap, vdeps, nk) in enumerate(blocks):
                        mm(po.ap[0:n, 0:128], pT_sb.ap[0:nk, kb, 0:n], vap, kb == 0, kb == len(blocks) - 1, [pT_sb] + vdeps, [po])
                    S.op("dve", lambda e, n=n: e.reciprocal(out=stt_.ap[0:n, 7:8], in_=stt_.ap[0:n, 6:7]), [stt_], [stt_])
                    ts(o_out[t].ap[0:n, h * 128:(h + 1) * 128], po.ap[0:n, 0:128], stt_.ap[0:n, 7:8], 1.0, ALU.mult, ALU.mult, [po, stt_], [o_out[t]])

                pipelined = ps.kind == "p"
                for idx, (t, hh) in enumerate(order):
                    if pipelined:
                        if idx == 0:
                            stageA(0, t, hh)
                        if idx + 1 < len(order):
                            stageA(idx + 1, *order[idx + 1])
                        stageB(idx)
                    else:
                        stageA(idx, t, hh)
                        stageB(idx)
            if ps.kind == "p" and not os.environ.get("DBG_NOHIST"):
                for t in kv_t:
                    off, n = ps.tiles[t]
                    for hh in range(2):
                        cp(khist.ap[:, hp * 2 + hh, off:off + n], kTcur.ap[:, hh, off:off + n], [kTcur], [khist], eng="dve")
                    act(vhist.ap[:, t, hp * 256:(hp + 1) * 256], vcur[t].ap[:, :], AF.Copy, [vcur[t]], [vhist])

    dma("sp", ctab.ap[:, :], ctab_d[:, :], [], [ctab])
    cp(identb.ap[:, :], ctab.ap[:, 256:384], [ctab], [identb])
    dma("sp", ngc.ap[:, :, :], ng_d.rearrange("g p k -> p g k"), [], [ngc])
    dma("sp", w2_sb.ap[:, :], w2_d[:, :], [], [w2_sb])
    dma("sp", gateb.ap[:, :], gateb_d[0:1, :].partition_broadcast(128), [], [gateb])
    dma("sp", ngb[0].ap[:, :], retg_d[0:1, :].partition_broadcast(128), [], [ngb[0]])
    dma("sp", ngb[1].ap[:, :], glag_d[0:1, :].partition_broadcast(128), [], [ngb[1]])
    for l in range(2):
        for w_ in range(3):
            dma("sp", cw[l].ap[:, w_, :], cw_d[l, w_].rearrange("(fb p) -> p fb", p=128), [], [cw[l]], slow=True)
        dma("sp", cbt[l].ap[:, :], cb_d[l].rearrange("(fb p) -> p fb", p=128), [], [cbt[l]], slow=True)
        S.op("dve", lambda e, l=l: e.memset(ghist[l].ap[:, :, :], 0.0), [], [ghist[l]])
        for s in range(4):
            for r in range(2):
                dma("sp", ghs_in[l][s].ap[:, :, r], st_conv_d[l, s, r].rearrange("(fb p) -> p fb", p=128), [], [ghs_in[l][s]], slow=True)
    for h in range(4):
        S.op("dve", lambda e, h=h: e.memset(Sf_ret[h].ap[:, :], 0.0), [], [Sf_ret[h]])
        S.op("dve", lambda e, h=h: e.memset(Sf_gla[h].ap[:, :], 0.0), [], [Sf_gla[h]])
    S.op("pool", lambda e: e.memset(khist.ap[:, :, :], 0.0), [], [khist])
    S.op("pool", lambda e: e.memset(vhist.ap[:, :, :], 0.0), [], [vhist])
    S.op("dve", lambda e: e.memset(acc.ap[0:16, 0:319], 0.0), [], [acc])
    dma("sp", erb.ap[:, 319:LB], rbr_d[:, 0:320], [], [erb])
    ts(erb.ap[:, 0:319], acc.ap[0:16, 0:319], erb.ap[:, 319:320], 0.0, ALU.add, ALU.add, [acc, erb], [erb])
    for r in range(64):
        dstF = bass.AP(tensor=fsc_h, offset=r * LB, ap=[[64 * LB + 64, 16], [1, LB]])
        dma("sp", dstF, erb.ap[:, :], [erb], [fsc_d])

    passes = [
        Pass("p", x0=0, l0_rest=[3], l1_kv=[3], l1_attn=[], l1_rest=None, first=True),
        Pass("p", x0=512, l1_attn=[3], l1_rest="hist"),
        Pass("p", x0=1024, maskable=True, yrow=0),
        Pass("p", x0=1536, out=True, yrow=512),
        Pass("s"),
    ]
    passes = passes[:npass] if npass < 5 else passes
    if npass < 5:
        passes[-1].out = True
    for pi, ps in enumerate(passes):
        T_ = ps.T
        src = xp_d if ps.kind == "p" else xs_d
        for t, (off, n) in enumerate(ps.xtiles):
            dma("sp", xt[t].ap[0:n, :], src[ps.x0 + off:ps.x0 + off + n, :], [], [xt[t]])
        if ps.kind == "p":
            dma("sp", rope.ap[:, :, 0:T_], rope_p_d[:, :, ps.x0:ps.x0 + T_], [], [rope])
        else:
            dma("sp", rope.ap[:, :, 0:T_], rope_s_d[:, :, :], [], [rope])
        norm_hT(ps, 0)
        l0_mixer(ps)
        if stop_after == "mixer0":
            continue
        r0 = ps.l0_rest
        o_to_hT(ps, r0)
        out_proj(ps, w_oab_d, r0)
        norm_hT(ps, 1, r0)
        ffn(ps, 0, r0)
        if stop_after == "ffn0":
            continue
        norm_hT(ps, 2, ps.l1_kv)
        if stop_after == "l1norm":
            continue
        if stop_after == "l1kv":
            ps.l1_attn = []
        l1_attn(ps)
        if ps.l1_rest is None or stop_after in ("l1kv", "attn"):
            continue
        r1 = ps.l1_attn
        o_to_hT(ps, r1)
        out_proj(ps, w_oat_d, r1)
        if stop_after == "l1out":
            continue
        norm_hT(ps, 3, r1)
        if ps.l1_rest == "hist":
            ffn(ps, 1, r1, hist_only=True)
            continue
        ffn(ps, 1, r1)
        if stop_after == "ffn1":
            continue
        yd = yp_d if ps.kind == "p" else ys_d
        cnt["w"] += 1
        gfw = wbs[cnt["w"] % NW]
        gfin_ap = gfw.ap[:, :].bitcast(F32)
        dma("sp", gfin_ap, gfin_d[0:1, :].partition_broadcast(128), [], [gfw])
        for t, (off, n) in enumerate(ps.xtiles):
            act(hb.ap[0:n, :], xt[t].ap[0:n, :], AF.Square, [xt[t]], [hb, st], accum_out=st.ap[0:n, 0:1])
            rstd_from(st.ap[0:n, 0:1], n, 1.0 / D)
            stt(xt[t].ap[0:n, :], xt[t].ap[0:n, :], st.ap[0:n, 3:4], gfin_ap[0:n, :], ALU.mult, ALU.mult, [xt[t], st, gfw], [xt[t]])
            dma("sp", yd[ps.yrow + off:ps.yrow + off + n, :], xt[t].ap[0:n, :], [xt[t]], [])
    S.final_wait_all()
    S.emit(nc)
    es.close()
    return nc


def _consts(half):
    i = np.arange(128, dtype=np.float64)
    inv = 10000.0 ** (-i / 128.0)

    def cs(pos):
        ang = inv[:, None] * np.asarray(pos, dtype=np.float64)[None, :]
        return np.stack([np.cos(ang), np.sin(ang)], axis=1).astype(np.float32)

    if half == 1:
        pos_p = np.arange(2048)
    else:
        pos_p = np.concatenate([np.zeros(1024), np.arange(1024)])
    rope_p = cs(pos_p)
    rope_s = cs(np.tile(2048 + np.arange(32), 4))
    gam = np.array([1.0 - 2.0 ** (-5 - h) for h in range(4)], dtype=np.float64)

    def dt(C, Tn):
        tpos = (np.arange(Tn) % C) + 1
        out = np.zeros((128, 8, Tn), np.float32)
        for h in range(4):
            out[:, 2 * h, :] = (gam[h] ** tpos)[None, :]
            out[:, 2 * h + 1, :] = (gam[h] ** (-tpos) * 256.0 ** -0.5)[None, :]
        return out

    ctab = np.zeros((128, 385), np.float32)
    jj, ii = np.meshgrid(np.arange(128), np.arange(128), indexing="ij")
    ctab[:, 0:128] = (ii >= jj).astype(np.float32)
    ctab[:, 128:256] = (ii >= jj).astype(np.float32) / 16.0
    ctab[:, 256:384] = np.eye(128, dtype=np.float32)
    ctab[:, 384] = 0.0 if half == 1 else NEG
    return dict(rope_p=rope_p, rope_s=rope_s, dtab_p=dt(128, 512), dtab_s=dt(32, 128), ctab=ctab)


_NC_CACHE = {}


def make_in_maps(inp):
    f = lambda a: np.ascontiguousarray(a, dtype=np.float32)
    def blk(w):
        nb = w.shape[1] // 256
        return f(w.reshape(16, 128, nb, 256).transpose(2, 1, 0, 3).reshape(nb, 128, 4096))

    def blkc(w):
        ncl = w.shape[1]
        return w.reshape(16, 128, ncl).transpose(1, 0, 2).reshape(128, 16 * ncl)

    def blkr(w, nr):
        return w.reshape(nr, 128, 512).transpose(1, 0, 2).reshape(128, nr * 512)

    w_in = inp["w_in_ab"][0]
    w_up = inp["w_ffn_up"]
    w_dn = inp["w_ffn_down"]
    up2 = np.zeros((2, 4, 5, 2, 128, 4096), np.float32)
    up1 = np.zeros((2, 4, 2, 128, 2048), np.float32)
    dnA = np.zeros((2, 4, 4, 128, 4096), np.float32)
    dnB = np.zeros((2, 4, 4, 128, 1536), np.float32)
    for l in range(2):
        for q in range(4):
            f0 = q * 11
            for i in range(5):
                c0 = (f0 + 2 * i) * 128
                up2[l, q, i, 0] = blkc(w_up[l][:, c0:c0 + 256])
                up2[l, q, i, 1] = blkc(w_up[l][:, DFF + c0:DFF + c0 + 256])
            c0 = (f0 + 10) * 128
            up1[l, q, 0] = blkc(w_up[l][:, c0:c0 + 128])
            up1[l, q, 1] = blkc(w_up[l][:, DFF + c0:DFF + c0 + 128])
            for cb in range(4):
                dnA[l, q, cb] = blkr(w_dn[l][f0 * 128:(f0 + 8) * 128, cb * 512:(cb + 1) * 512], 8)
                dnB[l, q, cb] = blkr(w_dn[l][(f0 + 8) * 128:(f0 + 11) * 128, cb * 512:(cb + 1) * 512], 3)
    shared = dict(
        w_in_blk=blk(w_in[:, 0:7168]), w_lo=f(blkc(w_in[:, 7168:7184])),
        gate_w2=f(inp["gla_gate_w2"][0]), gate_b=f(inp["gla_gate_b"]),
        ret_g=f(inp["ret_norm_g"]), gla_g=f(inp["gla_norm_g"]), w_out_ab=blk(inp["w_out_ab"][0]),
        w_qkv=blk(inp["w_qkv_att"][0]), rbr=f(inp["rel_bias_att"][0][:, ::-1]), w_out_att=blk(inp["w_out_att"][0]),
        norm_final_g=f(inp["norm_final_g"][None, :]), w_up2=up2, w_up1=up1, conv_w=f(inp["ffn_conv_w"]),
        conv_b=f(inp["ffn_conv_b"]), w_dnA=dnA, w_dnB=dnB,
    )
    ng = np.stack([inp["norm_mix_g"][0], inp["norm_ffn_g"][0], inp["norm_mix_g"][1], inp["norm_ffn_g"][1],
                   inp["norm_final_g"]], axis=0)
    shared["norm_g"] = f(ng.reshape(5, 16, 128).transpose(0, 2, 1))
    consts = [_consts(0), _consts(1)]
    maps = []
    for c in range(8):
        seq, half = c // 2, c % 2
        xs = inp["x_prompt"][seq]
        if half == 1:
            xp = f(xs)
        else:
            xp = f(np.concatenate([np.zeros((1024, D), np.float32), xs[:1024]], axis=0))
        sl = slice(4 * c, 4 * c + 4)
        m = dict(shared)
        m.update(consts[half])
        m.update(
            xp=xp, xs=f(inp["x_sample"][sl].reshape(128, D)),
            st_ret=f(inp["state_ret"][0, sl]), st_gla=f(inp["state_gla"][0, sl]),
            ck=f(inp["cache_attn_k"][0, sl].reshape(4, 512, D)), cv=f(inp["cache_attn_v"][0, sl].reshape(4, 512, D)),
            st_conv=f(inp["state_ffn_conv"][:, sl]),
        )
        maps.append(m)
    return maps


def assemble(results):
    y_prompt = np.zeros((4, 2048, D), np.float32)
    y_sample = np.zeros((32, 32, D), np.float32)
    p_ret = np.zeros((1, 4, 4, 256, 256), np.float32)
    p_gla = np.zeros((1, 4, 4, 128, 256), np.float32)
    p_k = np.zeros((1, 4, 512, 16, 128), np.float32)
    p_v = np.zeros((1, 4, 512, 16, 128), np.float32)
    p_conv = np.zeros((2, 4, 2, DFF), np.float32)
    s_ret = np.zeros((1, 32, 4, 256, 256), np.float32)
    s_gla = np.zeros((1, 32, 4, 128, 256), np.float32)
    s_k = np.zeros((1, 32, 32, 16, 128), np.float32)
    s_v = np.zeros((1, 32, 32, 16, 128), np.float32)
    s_conv = np.zeros((2, 32, 2, DFF), np.float32)
    for c in range(8):
        r = results[c]
        seq, half = c // 2, c % 2
        sl = slice(4 * c, 4 * c + 4)
        y_prompt[seq, half * 1024:(half + 1) * 1024] = r["yp"]
        y_sample[sl] = r["ys"].reshape(4, 32, D)
        s_ret[0, sl] = r["s_ret"]
        s_gla[0, sl] = r["s_gla"]
        s_k[0, sl] = r["s_k"].reshape(4, 32, 16, 128)
        s_v[0, sl] = r["s_v"].reshape(4, 32, 16, 128)
        s_conv[:, sl] = r["s_conv"]
        if half == 1:
            p_ret[0, seq] = r["p_ret"]
            p_gla[0, seq] = r["p_gla"]
            p_k[0, seq] = r["p_k"].reshape(512, 16, 128)
            p_v[0, seq] = r["p_v"].reshape(512, 16, 128)
            p_conv[:, seq] = r["p_conv"]
    return (y_prompt, y_sample, p_ret, p_gla, p_k, p_v, p_conv, s_ret, s_gla, s_k, s_v, s_conv)


def kernel(**inputs):
    inp = {k: np.asarray(v) for k, v in inputs.items()}
    if "nc" not in _NC_CACHE:
        _NC_CACHE["nc"] = build_program()
    nc = _NC_CACHE["nc"]
    maps = make_in_maps(inp)
    res = run_bass_kernel_spmd(nc, maps, core_ids=list(range(8)))
    return assemble(res.results)
```

```python
import os
import numpy as np
from contextlib import ExitStack
import concourse.bass as bass
import concourse.mybir as mybir
from concourse.bass_utils import run_bass_kernel_spmd

F32 = mybir.dt.float32
BF = mybir.dt.bfloat16
AF = mybir.ActivationFunctionType
ALU = mybir.AluOpType
AX = mybir.AxisListType

D = 2048
DFF = 5632
NFB = DFF // 128
EPS = 1e-6
ENGINES = ("pe", "act", "dve", "pool", "sp")
SEM_LIMIT = 8000
NDSEM = 24
NEG = -1e30
LB = 639


class Dep:
    __slots__ = ("w", "r")

    def __init__(self):
        self.w = None
        self.r = []


class T:
    def __init__(self, ap, dep=None, deps=None):
        self.ap = ap
        if deps is not None:
            self.deps = list(deps)
        else:
            self.deps = [dep if dep is not None else Dep()]
        self.dep = self.deps[0]

    def __getitem__(self, idx):
        return self.ap[idx]


class Sched:
    def __init__(self):
        self.ops = {e: [] for e in ENGINES}
        self.tick = {e: 0 for e in ENGINES}
        self.dtick = {e: 0 for e in ENGINES}
        self.seen = {e: {} for e in ENGINES}
        self.semkeys = set()

    def _ckey(self, e, tick):
        ep = (tick - 1) // SEM_LIMIT
        k = (e, "c", ep)
        self.semkeys.add(k)
        return k, tick - ep * SEM_LIMIT

    def _dkey(self, e, dt):
        ep = (dt - 16) // SEM_LIMIT
        k = (e, "d", ep)
        self.semkeys.add(k)
        return k, dt - ep * SEM_LIMIT

    def _need(self, e, reads, writes, skip_self=False):
        need = {}

        def add(kv):
            if kv is None:
                return
            k, v = kv
            if skip_self and k[0] == e and k[1] == "c":
                return
            if need.get(k, 0) < v:
                need[k] = v

        for d in reads:
            for dp in d.deps:
                add(dp.w)
        for d in writes:
            for dp in d.deps:
                add(dp.w)
                for kv in dp.r:
                    add(kv)
        waits = []
        seen = self.seen[e]
        for k, v in need.items():
            if seen.get(k, 0) < v:
                seen[k] = v
                waits.append((k, v))
        return waits

    def _mark(self, kv, reads, writes):
        for d in reads:
            for dp in d.deps:
                dp.r.append(kv)
        for d in writes:
            for dp in d.deps:
                dp.w = kv
                dp.r = []

    def op(self, e, fn, reads=(), writes=()):
        waits = self._need(e, reads, writes, skip_self=(e == "pe"))
        self.tick[e] += 1
        kv = self._ckey(e, self.tick[e])
        self.ops[e].append((waits, fn, kv[0], 1))
        self._mark(kv, reads, writes)

    def dma(self, e, fn, reads=(), writes=()):
        waits = self._need(e, reads, writes)
        if not hasattr(self, "dslot"):
            self.dslot = {q: 0 for q in ENGINES}
            self.dcnt = {q: [0] * NDSEM for q in ENGINES}
        j = self.dslot[e]
        self.dslot[e] = (j + 1) % NDSEM
        k = (e, "d", j)
        self.semkeys.add(k)
        prev = self.dcnt[e][j]
        if prev and self.seen[e].get(k, 0) < prev:
            self.seen[e][k] = prev
            waits.append((k, prev))
        self.dcnt[e][j] = prev + 16
        self.dtick[e] += 16
        kv = (k, prev + 16)
        self.ops[e].append((waits, fn, k, 16))
        self._mark(kv, reads, writes)

    def _dcur(self):
        cur = []
        if hasattr(self, "dslot"):
            for q in ENGINES:
                for j in range(NDSEM):
                    if self.dcnt[q][j]:
                        cur.append(((q, "d", j), self.dcnt[q][j]))
        return cur

    def barrier(self):
        cur = []
        for e in ENGINES:
            if self.tick[e]:
                cur.append(self._ckey(e, self.tick[e]))
        cur += self._dcur()
        for e in ENGINES:
            waits = []
            for k, v in cur:
                if k[0] == e and k[1] == "c":
                    continue
                if self.seen[e].get(k, 0) < v:
                    self.seen[e][k] = v
                    waits.append((k, v))
            if waits:
                self.ops[e].append((waits, None, None, 0))

    def final_wait_all(self):
        cur = []
        for e in ENGINES:
            if self.tick[e] and e != "sp":
                cur.append(self._ckey(e, self.tick[e]))
        cur += self._dcur()
        self.ops["sp"].append((cur, None, None, 0))

    def emit(self, nc):
        keys = sorted(self.semkeys)
        with ExitStack() as es:
            sems = {}
            for k in keys:
                sems[k] = es.enter_context(nc.semaphore("s_%s_%s_%d" % k))
            block = es.enter_context(nc.Block())

            def run(eng, e):
                for waits, fn, k, inc in self.ops[e]:
                    for wk, wv in waits:
                        eng.wait_ge(sems[wk], wv)
                    if fn is not None:
                        fn(eng).then_inc(sems[k], inc)

            @block.tensor
            def _(eng):
                run(eng, "pe")

            @block.scalar
            def _(eng):
                run(eng, "act")

            @block.vector
            def _(eng):
                run(eng, "dve")

            @block.gpsimd
            def _(eng):
                run(eng, "pool")

            @block.sync
            def _(eng):
                run(eng, "sp")


class Pass:
    def __init__(self, kind, x0=0, l0_rest=(0, 1, 2, 3), l1_kv=(0, 1, 2, 3), l1_attn=(0, 1, 2, 3), l1_rest="full",
                 out=False, maskable=False, yrow=0, first=False):
        self.kind = kind
        self.x0 = x0
        self.l0_rest = list(l0_rest)
        self.l1_kv = list(l1_kv)
        self.l1_attn = list(l1_attn)
        self.l1_rest = l1_rest
        self.out = out
        self.maskable = maskable
        self.yrow = yrow
        self.first = first
        if kind == "p":
            self.tiles = [(i * 128, 128) for i in range(4)]
            self.T = 512
            self.C = 128
        else:
            self.tiles = [(i * 32, 32) for i in range(4)]
            self.T = 128
            self.C = 32
        self.xtiles = self.tiles if kind == "p" else [(0, 128)]

    def xsel(self, tsel):
        return list(tsel) if self.kind == "p" else [0]


def build_program(stop_after=None, npass=5):
    nc = bass.Bass("TRN2", target_bir_lowering=False)
    S = Sched()
    es = ExitStack()

    def din(name, shape):
        return nc.dram_tensor(name, list(shape), F32, kind="ExternalInput").ap()

    def dout(name, shape):
        return nc.dram_tensor(name, list(shape), F32, kind="ExternalOutput").ap()

    xp_d = din("xp", [2048, D])
    xs_d = din("xs", [128, D])
    st_ret_d = din("st_ret", [4, 4, 256, 256])
    st_gla_d = din("st_gla", [4, 4, 128, 256])
    ck_d = din("ck", [4, 512, D])
    cv_d = din("cv", [4, 512, D])
    st_conv_d = din("st_conv", [2, 4, 2, DFF])
    w_in_d = din("w_in_blk", [28, 128, 4096])
    w_lo_d = din("w_lo", [128, 256])
    w2_d = din("gate_w2", [16, 512])
    gateb_d = din("gate_b", [1, 512])
    retg_d = din("ret_g", [1, 256])
    glag_d = din("gla_g", [1, 256])
    w_oab_d = din("w_out_ab", [8, 128, 4096])
    w_qkv_d = din("w_qkv", [24, 128, 4096])
    rbr_d = din("rbr", [16, 513])
    w_oat_d = din("w_out_att", [8, 128, 4096])
    ng_d = din("norm_g", [5, 128, 16])
    gfin_d = din("norm_final_g", [1, D])
    w_up2_d = din("w_up2", [2, 4, 5, 2, 128, 4096])
    w_up1_d = din("w_up1", [2, 4, 2, 128, 2048])
    cw_d = din("conv_w", [2, 3, DFF])
    cb_d = din("conv_b", [2, DFF])
    w_dnA_d = din("w_dnA", [2, 4, 4, 128, 4096])
    w_dnB_d = din("w_dnB", [2, 4, 4, 128, 1536])
    ctab_d = din("ctab", [128, 385])
    rope_p_d = din("rope_p", [128, 2, 2048])
    rope_s_d = din("rope_s", [128, 2, 128])
    dtab_p_d = din("dtab_p", [128, 8, 512])
    dtab_s_d = din("dtab_s", [128, 8, 128])

    yp_d = dout("yp", [1024, D])
    ys_d = dout("ys", [128, D])
    p_ret_d = dout("p_ret", [4, 256, 256])
    p_gla_d = dout("p_gla", [4, 128, 256])
    p_k_d = dout("p_k", [512, D])
    p_v_d = dout("p_v", [512, D])
    p_conv_d = dout("p_conv", [2, 2, DFF])
    s_ret_d = dout("s_ret", [4, 4, 256, 256])
    s_gla_d = dout("s_gla", [4, 4, 128, 256])
    s_k_d = dout("s_k", [128, D])
    s_v_d = dout("s_v", [128, D])
    s_conv_d = dout("s_conv", [2, 4, 2, DFF])
    fsc_h = nc.dram_tensor("fscratch", [16, 64 * LB + 64], F32, kind="Internal")
    fsc_d = T(fsc_h.ap())

    def sb(name, shape, dt):
        return T(es.enter_context(nc.sbuf_tensor("sb_" + name, list(shape), dt)))

    def psum(name, shape, dt):
        return T(es.enter_context(nc.psum_tensor("ps_" + name, list(shape), dt)))

    pfs = [psum("pf%d" % i, [128, 512], F32) for i in range(4)]
    psc = psum("psc", [128, 1024], F32)
    psc = T(psc.ap, deps=[Dep(), Dep()])
    pfs.append(T(psc.ap[:, 0:512], psc.deps[0]))
    pfs.append(T(psc.ap[:, 512:1024], psc.deps[1]))
    pbs = [psum("pb%d" % i, [128, 8, 128], BF) for i in range(2)]
    cnt = {"pf": 0, "pb": 0, "w": 0}

    def PF():
        cnt["pf"] += 1
        return pfs[cnt["pf"] % 6]

    def PB():
        cnt["pb"] += 1
        return pbs[cnt["pb"] % 2]

    ctab = sb("ctab", [128, 385], F32)
    maskT = ctab.ap[:, 0:128]
    tri = ctab.ap[:, 128:256]
    maskval = ctab.ap[:, 384:385]
    identb = sb("identb", [128, 128], BF)
    xt = [sb("x%d" % i, [128, D], F32) for i in range(4)]
    hT = sb("hT", [128, 16, 512], BF)
    NW = 5
    wbs = [sb("w%d" % i, [128, 4096], BF) for i in range(NW)]
    st = sb("st", [128, 8], F32)
    ngc = sb("ngc", [128, 5, 16], F32)
    rope = sb("rope", [128, 2, 512], F32)
    dtab = sb("dtab", [128, 2, 512], F32)
    Sf_ret = [sb("sfr%d" % h, [128, 512], F32) for h in range(4)]
    Sf_gla = [sb("sfg%d" % h, [128, 256], F32) for h in range(4)]
    Sb = sb("Sb", [128, 512], BF)
    ghist = [sb("ghist%d" % l, [128, NFB, 2], F32) for l in range(2)]
    cw = [sb("cw%d" % l, [128, 3, NFB], F32) for l in range(2)]
    cbt = [sb("cb%d" % l, [128, NFB], F32) for l in range(2)]
    khist = sb("khist", [128, 16, 512], BF)
    vhist = sb("vhist", [128, 4, D], BF)
    oo = sb("oo", [128, 4 * D], BF)
    o_out = [T(oo.ap[:, i * D:(i + 1) * D]) for i in range(4)]
    actT = T(oo.ap[:, 0:11 * 512].rearrange("p (f t) -> p f t", t=512))
    hb = T(oo.ap[:, 0:D], o_out[0].dep)
    tmpall = sb("tmpall", [128, 2048], F32)
    tmp = [T(tmpall.ap[:, i * 512:(i + 1) * 512]) for i in range(4)]
    qh = sb("qh", [128, 2, 512], BF)
    kh = sb("kh", [128, 2, 512], BF)
    kst = sb("kst", [128, 512], BF)
    v_sb = [sb("v%d" % i, [128, 256], BF) for i in range(4)]
    g_sb = [sb("g%d" % i, [128, 256], F32) for i in range(4)]
    scT = sb("scT", [128, 128], BF)
    kT_sb = sb("kT_sb", [128, 2, 128], BF)
    ogt = sb("ogt", [128, 256], F32)
    ngb = [sb("ngb%d" % i, [128, 256], F32) for i in range(2)]
    lo_sb = sb("lo_sb", [16, 512], F32)
    w2_sb = sb("w2_sb", [16, 512], F32)
    gateb = sb("gateb", [128, 512], F32)
    la = [sb("la%d" % i, [128, 512], F32) for i in range(4)]
    Eq = sb("Eq", [128, 512], F32)
    Ek = sb("Ek", [128, 512], F32)
    gext = T(rope.ap[:, :, :].rearrange("p a t -> p (a t)")[:, 0:520], rope.dep)
    acc = T(dtab.ap[:, 0, :], dtab.dep)
    ghs_in = [[sb("ghsi%d_%d" % (l, s), [128, NFB, 2], F32) for s in range(4)] for l in range(2)]
    ghs_out = ghs_in
    qT = qh
    kcur = v_sb
    vcur = [T(g_sb[i].ap[:, :].bitcast(BF)[:, 0:256], g_sb[i].dep) for i in range(4)]
    kTcur = kh
    kvst = [ogt] * 2
    bias2 = [T(tmpall.ap[:, 0:640], deps=[tmp[0].dep, tmp[1].dep]),
             T(tmpall.ap[:, 640:1280], deps=[tmp[1].dep, tmp[2].dep])]
    s_sb = T(tmpall.ap[:, 1280:1920], deps=[tmp[2].dep, tmp[3].dep])
    p_bf = T(Eq.ap[:, :].bitcast(BF)[:, 0:640], Eq.dep)
    p_bfs = [p_bf, T(la[3].ap[:, :].bitcast(BF)[:, 0:640], la[3].dep)]
    sts = [sb("stA", [128, 8], F32), sb("stB", [128, 8], F32)]
    pT_sb = T(Ek.ap[:, :].bitcast(BF)[:, 0:640].rearrange("p (a c) -> p a c", c=128), Ek.dep)
    kc_tm = T(la[0].ap[:, :].bitcast(BF).rearrange("p (a c) -> p a c", c=256), la[0].dep)
    vc_tm = T(la[1].ap[:, :].bitcast(BF).rearrange("p (a c) -> p a c", c=256), la[1].dep)
    kTc = T(la[2].ap[:, :].bitcast(BF).rearrange("p (a c) -> p a c", c=128), la[2].dep)
    erb = T(s_sb.ap[0:16, 0:LB], deps=s_sb.deps)

    def mm(out, lhsT, rhs, start, stop, reads, writes):
        S.op("pe", lambda e: e.matmul(out, lhsT=lhsT, rhs=rhs, start=start, stop=stop), reads, writes)

    def tr(out, in_, ident, reads, writes):
        S.op("pe", lambda e: e.transpose(out=out, in_=in_, identity=ident), list(reads) + [identb], writes)

    def act(out, in_, func, reads, writes, **kw):
        S.op("act", lambda e: e.activation(out=out, in_=in_, func=func, **kw), reads, writes)

    def tt(out, in0, in1, op, reads, writes, eng="dve"):
        S.op(eng, lambda e: e.tensor_tensor(out=out, in0=in0, in1=in1, op=op), reads, writes)

    def ts(out, in0, s1, s2, op0, op1, reads, writes, eng="dve"):
        S.op(eng, lambda e: e.tensor_scalar(out=out, in0=in0, scalar1=s1, scalar2=s2, op0=op0, op1=op1), reads, writes)

    def stt(out, in0, scalar, in1, op0, op1, reads, writes, eng="dve"):
        S.op(eng, lambda e: e.scalar_tensor_tensor(out=out, in0=in0, scalar=scalar, in1=in1, op0=op0, op1=op1), reads, writes)

    def cp(out, in_, reads, writes, eng="dve"):
        S.op(eng, lambda e: e.tensor_copy(out=out, in_=in_), reads, writes)

    def dma(q, out, in_, reads, writes, slow=False):
        if slow:
            S.dma(q, lambda e: e.dma_start(out=out, in_=in_, allow_slow_non_contiguous=True), reads, writes)
        else:
            S.dma(q, lambda e: e.dma_start(out=out, in_=in_), reads, writes)

    def load_w(src2, nel):
        cnt["w"] += 1
        w = wbs[cnt["w"] % NW]
        dma("pool", w.ap[:, 0:nel], src2, [], [w])
        return w

    def rstd_from(col_ss, n, inv_n):
        ts(st.ap[0:n, 1:2], col_ss, inv_n, EPS, ALU.mult, ALU.add, [st], [st])
        act(st.ap[0:n, 2:3], st.ap[0:n, 1:2], AF.Sqrt, [st], [st])
        S.op("dve", lambda e: e.reciprocal(out=st.ap[0:n, 3:4], in_=st.ap[0:n, 2:3]), [st], [st])

    def norm_hT(ps, gi, tsel=(0, 1, 2, 3)):
        for t in ps.xsel(tsel):
            off, n = ps.xtiles[t]
            act(hb.ap[0:n, :], xt[t].ap[0:n, :], AF.Square, [xt[t]], [hb, st], accum_out=st.ap[0:n, 0:1])
            rstd_from(st.ap[0:n, 0:1], n, 1.0 / D)
            ts(hb.ap[0:n, :], xt[t].ap[0:n, :], st.ap[0:n, 3:4], 1.0, ALU.mult, ALU.mult, [xt[t], st], [hb])
            for g8 in range(2):
                pb = PB()
                for j in range(8):
                    kc = g8 * 8 + j
                    tr(pb.ap[:, j, 0:n], hb.ap[0:n, kc * 128:(kc + 1) * 128], identb.ap[0:n, 0:n], [hb], [pb])
                for j in range(8):
                    kc = g8 * 8 + j
                    eng = "dve" if j % 2 == 0 else "pool"
                    if eng == "pool":
                        act(hT.ap[:, kc, off:off + n], pb.ap[:, j, 0:n], AF.Copy, [pb, ngc], [hT], scale=ngc.ap[:, gi, kc:kc + 1])
                    else:
                        ts(hT.ap[:, kc, off:off + n], pb.ap[:, j, 0:n], ngc.ap[:, gi, kc:kc + 1], 1.0, ALU.mult, ALU.mult, [pb, ngc], [hT])

    def o_to_hT(ps, tsel=(0, 1, 2, 3)):
        for t in tsel:
            off, n = ps.tiles[t]
            for g8 in range(2):
                pb = PB()
                for j in range(8):
                    kc = g8 * 8 + j
                    tr(pb.ap[:, j, 0:n], o_out[t].ap[0:n, kc * 128:(kc + 1) * 128], identb.ap[0:n, 0:n], [o_out[t]], [pb])
                if g8 == 0:
                    act(hT.ap[:, 0:8, off:off + n], pb.ap[:, :, 0:n], AF.Copy, [pb], [hT])
                else:
                    cp(hT.ap[:, 8:16, off:off + n], pb.ap[:, :, 0:n], [pb], [hT])

    def out_proj(ps, w_d, tsel=(0, 1, 2, 3)):
        for cb in range(8):
            w = load_w(w_d[cb], 4096)
            for t in ps.xsel(tsel):
                off, n = ps.xtiles[t]
                p = PF()
                for kc in range(16):
                    mm(p.ap[0:n, 0:256], hT.ap[:, kc, off:off + n], w.ap[:, kc * 256:(kc + 1) * 256], kc == 0, kc == 15, [hT, w], [p])
                xs_ = xt[t].ap[0:n, cb * 256:(cb + 1) * 256]
                tt(xs_, xs_, p.ap[0:n, 0:256], ALU.add, [xt[t], p], [xt[t]])

    def vg_proj(ps, cv0, cg0):
        wv = load_w(w_in_d[cv0 // 256], 4096)
        wg = load_w(w_in_d[cg0 // 256], 4096)
        for t, (off, n) in enumerate(ps.tiles):
            p = PF()
            for kc in range(16):
                mm(p.ap[0:n, 0:256], hT.ap[:, kc, off:off + n], wv.ap[:, kc * 256:(kc + 1) * 256], kc == 0, kc == 15, [hT, wv], [p])
            for kc in range(16):
                mm(p.ap[0:n, 256:512], hT.ap[:, kc, off:off + n], wg.ap[:, kc * 256:(kc + 1) * 256], kc == 0, kc == 15, [hT, wg], [p])
            act(v_sb[t].ap[0:n, :], p.ap[0:n, 0:256], AF.Copy, [p], [v_sb[t]])
            act(g_sb[t].ap[0:n, :], p.ap[0:n, 256:512], AF.Silu, [p], [g_sb[t]])

    def recur(ps, ndc, Sf, st_in, st_out, p_out, kscale, ngt, ocol0, gamma_c=None, elast=False):
        C = ps.C
        for t, (off, n) in enumerate(ps.tiles):
            W_ = ndc * 256
            if ps.kind == "s":
                dma("sp", Sf.ap[:, 0:W_].rearrange("p (c v) -> p c v", v=256),
                    st_in[t].rearrange("(c p) v -> p c v", p=128), [], [Sf])
            pS = PF()
            for dc in range(ndc):
                mm(pS.ap[0:n, 0:n], kh.ap[:, dc, off:off + n], qh.ap[:, dc, off:off + n], dc == 0, dc == ndc - 1, [kh, qh], [pS])
            pb = PB()
            ksrc = kst if elast else kh
            for dc in range(ndc):
                src = ksrc.ap[:, off:off + n] if elast else ksrc.ap[:, dc, off:off + n]
                tr(pb.ap[0:n, dc, :], src, identb.ap[:, :], [ksrc], [pb])
            tt(scT.ap[0:n, 0:n], pS.ap[0:n, 0:n], maskT[0:n, 0:n], ALU.mult, [pS, ctab], [scT])
            act(kT_sb.ap[0:n, 0:ndc, :], pb.ap[0:n, 0:ndc, :], AF.Copy, [pb], [kT_sb], scale=float(kscale))
            act(Sb.ap[:, 0:W_], Sf.ap[:, 0:W_], AF.Copy, [Sf], [Sb])
            pO = PF()
            mm(pO.ap[0:n, 0:256], scT.ap[0:n, 0:n], v_sb[t].ap[0:n, :], True, False, [scT, v_sb[t]], [pO])
            for dc in range(ndc):
                mm(pO.ap[0:n, 0:256], qh.ap[:, dc, off:off + n], Sb.ap[:, dc * 256:(dc + 1) * 256], False, dc == ndc - 1, [qh, Sb], [pO])
            pD = PF()
            for dc in range(ndc):
                mm(pD.ap[:, dc * 256:(dc + 1) * 256], kT_sb.ap[0:n, dc, :], v_sb[t].ap[0:n, :], True, True, [kT_sb, v_sb[t]], [pD])
            if elast:
                sc_ = Eq.ap[:, off + n - 1:off + n]
                stt(Sf.ap[:, 0:W_], Sf.ap[:, 0:W_], sc_, pD.ap[:, 0:W_], ALU.mult, ALU.add, [Sf, Eq, pD], [Sf])
            else:
                stt(Sf.ap[:, 0:W_], Sf.ap[:, 0:W_], float(gamma_c), pD.ap[:, 0:W_], ALU.mult, ALU.add, [Sf, pD], [Sf])
            act(ogt.ap[0:n, :], pO.ap[0:n, 0:256], AF.Square, [pO], [ogt, st], accum_out=st.ap[0:n, 0:1])
            rstd_from(st.ap[0:n, 0:1], n, 1.0 / 256)
            stt(ogt.ap[0:n, :], pO.ap[0:n, 0:256], st.ap[0:n, 3:4], ngt.ap[0:n, :], ALU.mult, ALU.mult, [pO, st, ngt], [ogt])
            tt(o_out[t].ap[0:n, ocol0:ocol0 + 256], ogt.ap[0:n, :], g_sb[t].ap[0:n, :], ALU.mult, [ogt, g_sb[t]], [o_out[t]])
            if ps.kind == "s":
                dma("sp", st_out[t].rearrange("(c p) v -> p c v", p=128),
                    Sf.ap[:, 0:W_].rearrange("p (c v) -> p c v", v=256), [Sf], [])
        if ps.kind == "p" and ps.out:
            dma("sp", p_out.rearrange("(c p) v -> p c v", p=128),
                Sf.ap[:, 0:ndc * 256].rearrange("p (c v) -> p c v", v=256), [Sf], [])

    def l0_mixer(ps):
        T_ = ps.T
        dtab_d = dtab_p_d if ps.kind == "p" else dtab_s_d
        gam = [1.0 - 2.0 ** (-5 - h) for h in range(4)]
        cos_ = rope.ap[:, 0, 0:T_]
        sin_ = rope.ap[:, 1, 0:T_]
        for h in range(4):
            dma("sp", dtab.ap[:, :, 0:T_], dtab_d[:, 2 * h:2 * h + 2, 0:T_], [], [dtab])
            for di, c0, dst in ((0, h * 256, qh), (1, 1024 + h * 256, kh)):
                w = load_w(w_in_d[c0 // 256], 4096)
                p1 = PF()
                p2 = PF()
                for half, pp in ((0, p1), (1, p2)):
                    for kc in range(16):
                        mm(pp.ap[:, 0:T_], w.ap[:, kc * 256 + half * 128:kc * 256 + half * 128 + 128], hT.ap[:, kc, 0:T_], kc == 0, kc == 15, [hT, w], [pp])
                dec = dtab.ap[:, di, 0:T_]
                tA, tB, tC, tD = [x.ap[:, 0:T_] for x in tmp]
                tt(tA, p1.ap[:, 0:T_], dec, ALU.mult, [p1, dtab], [tmp[0]])
                tt(tB, p2.ap[:, 0:T_], dec, ALU.mult, [p2, dtab], [tmp[1]])
                tt(tC, tA, cos_, ALU.mult, [tmp[0], rope], [tmp[2]])
                tt(tD, tB, sin_, ALU.mult, [tmp[1], rope], [tmp[3]])
                tt(dst.ap[:, 0, 0:T_], tC, tD, ALU.subtract, [tmp[2], tmp[3]], [dst])
                tt(tC, tA, sin_, ALU.mult, [tmp[0], rope], [tmp[2]])
                tt(tD, tB, cos_, ALU.mult, [tmp[1], rope], [tmp[3]])
                tt(dst.ap[:, 1, 0:T_], tC, tD, ALU.add, [tmp[2], tmp[3]], [dst])
            vg_proj(ps, 2048 + h * 256, 3072 + h * 256)
            gC = gam[h] ** ps.C
            recur(ps, 2, Sf_ret[h],
                  [st_ret_d[s, h] for s in range(4)], [s_ret_d[s, h] for s in range(4)], p_ret_d[h],
                  gC, ngb[0], h * 256, gamma_c=gC)
        wlo = load_w(w_lo_d[:, :], 256)
        p = PF()
        for kc in range(16):
            mm(p.ap[0:16, 0:T_], wlo.ap[:, kc * 16:(kc + 1) * 16], hT.ap[:, kc, 0:T_], kc == 0, kc == 15, [hT, wlo], [p])
        act(lo_sb.ap[0:16, 0:T_], p.ap[0:16, 0:T_], AF.Copy, [p], [lo_sb])
        for t, (off, n) in enumerate(ps.tiles):
            p = PF()
            mm(p.ap[0:n, 0:512], lo_sb.ap[0:16, off:off + n], w2_sb.ap[0:16, :], True, True, [lo_sb, w2_sb], [p])
            tA, tB, tC = tmp[0].ap[0:n, :], tmp[1].ap[0:n, :], tmp[2].ap[0:n, :]
            tt(tA, p.ap[0:n, 0:512], gateb.ap[0:n, :], ALU.add, [p, gateb], [tmp[0]])
            act(tB, tA, AF.Abs, [tmp[0]], [tmp[1]])
            tt(tC, tA, tB, ALU.subtract, [tmp[0], tmp[1]], [tmp[2]])
            act(tB, tB, AF.Exp, [tmp[1]], [tmp[1]], scale=-1.0)
            act(tB, tB, AF.Ln, [tmp[1]], [tmp[1]], bias=1.0)
            stt(la[t].ap[0:n, :], tC, 0.5, tB, ALU.mult, ALU.subtract, [tmp[2], tmp[1]], [la[t]])
        for h in range(4):
            for t, (off, n) in enumerate(ps.tiles):
                p = PF()
                mm(p.ap[:, 0:n], la[t].ap[0:n, h * 128:(h + 1) * 128], tri[0:n, 0:n], True, True, [la[t], ctab], [p])
                act(Eq.ap[:, off:off + n], p.ap[:, 0:n], AF.Exp, [p], [Eq])
                act(Ek.ap[:, off:off + n], p.ap[:, 0:n], AF.Exp, [p], [Ek], scale=-1.0)
            cnt["w"] += 1
            w = wbs[cnt["w"] % NW]
            wv3 = w.ap[:, 0:4096].rearrange("p (k n) -> p k n", n=256)
            hoff = (h % 2) * 128
            dma("pool", wv3[:, :, 0:128], w_in_d[16 + h // 2].rearrange("p (k n) -> p k n", n=256)[:, :, hoff:hoff + 128], [], [w])
            dma("pool", wv3[:, :, 128:256], w_in_d[18 + h // 2].rearrange("p (k n) -> p k n", n=256)[:, :, hoff:hoff + 128], [], [w])
            pq = PF()
            pk = PF()
            for half, pp in ((0, pq), (1, pk)):
                for kc in range(16):
                    mm(pp.ap[:, 0:T_], w.ap[:, kc * 256 + half * 128:kc * 256 + half * 128 + 128], hT.ap[:, kc, 0:T_], kc == 0, kc == 15, [hT, w], [pp])
            stt(qh.ap[:, 0, 0:T_], pq.ap[:, 0:T_], float(128 ** -0.5), Eq.ap[:, 0:T_], ALU.mult, ALU.mult, [pq, Eq], [qh])
            tt(kh.ap[:, 0, 0:T_], pk.ap[:, 0:T_], Ek.ap[:, 0:T_], ALU.mult, [pk, Ek], [kh])
            for t, (off, n) in enumerate(ps.tiles):
                stt(kst.ap[:, off:off + n], pk.ap[:, off:off + n], Eq.ap[:, off + n - 1:off + n], Ek.ap[:, off:off + n],
                    ALU.mult, ALU.mult, [pk, Eq, Ek], [kst])
            vg_proj(ps, 5120 + h * 256, 6144 + h * 256)
            recur(ps, 1, Sf_gla[h],
                  [st_gla_d[s, h] for s in range(4)], [s_gla_d[s, h] for s in range(4)], p_gla_d[h],
                  1.0, ngb[1], 1024 + h * 256, elast=True)

    def ffn(ps, l, tsel=(0, 1, 2, 3), hist_only=False):
        if ps.kind == "p":
            tok0 = ps.tiles[tsel[0]][0]
            Lt = sum(ps.tiles[t][1] for t in tsel)
            segs = [(tok0, Lt, None)]
        else:
            tok0, Lt = 0, ps.T
            segs = [(off, n, s) for s, (off, n) in enumerate(ps.tiles)]
        S.barrier()
        for q in range(4):
            f0 = q * 11
            units = [(f0 + 2 * i, 2) for i in range(5)] + [(f0 + 10, 1)]
            for ui, (fb0, nsub) in enumerate(units):
                c0 = fb0 * 128
                ncl = nsub * 128
                srcg = w_up2_d[l, q, ui, 0] if nsub == 2 else w_up1_d[l, q, 0]
                srcu = w_up2_d[l, q, ui, 1] if nsub == 2 else w_up1_d[l, q, 1]
                wg = load_w(srcg, 16 * ncl)
                if not hist_only:
                    wu = load_w(srcu, 16 * ncl)
                for sub in range(nsub):
                    fb = fb0 + sub
                    fbl = fb - f0
                    pg = PF()
                    for kc in range(16):
                        mm(pg.ap[:, tok0:tok0 + Lt], wg.ap[:, kc * ncl + sub * 128:kc * ncl + sub * 128 + 128], hT.ap[:, kc, tok0:tok0 + Lt], kc == 0, kc == 15, [hT, wg], [pg])
                    if hist_only:
                        ts(ghist[l].ap[:, fb, :], pg.ap[:, tok0 + Lt - 2:tok0 + Lt], 1.0, 0.0, ALU.mult, ALU.add, [pg], [ghist[l]])
                        continue
                    pu = PF()
                    for kc in range(16):
                        mm(pu.ap[:, tok0:tok0 + Lt], wu.ap[:, kc * ncl + sub * 128:kc * ncl + sub * 128 + 128], hT.ap[:, kc, tok0:tok0 + Lt], kc == 0, kc == 15, [hT, wu], [pu])
                    for (off, L, s) in segs:
                        hsrc = ghist[l] if s is None else ghs_in[l][s]
                        hdst = ghist[l] if s is None else ghs_out[l][s]
                        act(gext.ap[:, 0:2], hsrc.ap[:, fb, :], AF.Copy, [hsrc], [gext])
                        act(gext.ap[:, 2:2 + L], pg.ap[:, off:off + L], AF.Copy, [pg], [gext])
                        ts(acc.ap[:, 0:L], gext.ap[:, 0:L], cw[l].ap[:, 0, fb:fb + 1], cbt[l].ap[:, fb:fb + 1], ALU.mult, ALU.add, [gext, cw[l], cbt[l]], [acc])
                        stt(acc.ap[:, 0:L], gext.ap[:, 1:L + 1], cw[l].ap[:, 1, fb:fb + 1], acc.ap[:, 0:L], ALU.mult, ALU.add, [gext, cw[l], acc], [acc])
                        stt(acc.ap[:, 0:L], gext.ap[:, 2:L + 2], cw[l].ap[:, 2, fb:fb + 1], acc.ap[:, 0:L], ALU.mult, ALU.add, [gext, cw[l], acc], [acc])
                        act(acc.ap[:, 0:L], acc.ap[:, 0:L], AF.Silu, [acc], [acc])
                        tt(actT.ap[:, fbl, off:off + L], acc.ap[:, 0:L], pu.ap[:, off:off + L], ALU.mult, [acc, pu], [actT])
                        act(hdst.ap[:, fb, :], gext.ap[:, L:L + 2], AF.Copy, [gext], [hdst])
            if hist_only:
                continue
            for cb in range(4):
                for (kb0, nr) in ((0, 8), (8, 3)):
                    w = load_w((w_dnA_d if kb0 == 0 else w_dnB_d)[l, q, cb], nr * 512)
                    for t in ps.xsel(tsel):
                        off, n = ps.xtiles[t]
                        for s_ in range(nr):
                            mm(pfs[t].ap[0:n, 0:512], actT.ap[:, kb0 + s_, off:off + n], w.ap[:, s_ * 512:(s_ + 1) * 512],
                               kb0 == 0 and s_ == 0, kb0 == 8 and s_ == nr - 1, [actT, w], [pfs[t]])
                for t in ps.xsel(tsel):
                    off, n = ps.xtiles[t]
                    xs_ = xt[t].ap[0:n, cb * 512:(cb + 1) * 512]
                    tt(xs_, xs_, pfs[t].ap[0:n, 0:512], ALU.add, [xt[t], pfs[t]], [xt[t]])
        S.barrier()
        if ps.kind == "p" and ps.out:
            for r in range(2):
                dma("sp", p_conv_d[l, r].rearrange("(fb p) -> p fb", p=128), ghist[l].ap[:, :, r], [ghist[l]], [], slow=True)
        if ps.kind == "s":
            for s in range(4):
                for r in range(2):
                    dma("sp", s_conv_d[l, s, r].rearrange("(fb p) -> p fb", p=128), ghs_out[l][s].ap[:, :, r], [ghs_out[l][s]], [], slow=True)

    def l1_attn(ps):
        T_ = ps.T
        kv_t = ps.l1_kv
        at_t = ps.l1_attn
        full = len(at_t) > 0
        if full:
            for i in range(2):
                S.op("dve", lambda e, i=i: e.memset(bias2[i].ap[:, :], NEG), [], [bias2[i]])
        for hp in range(8):
            if full:
                q0 = ps.tiles[at_t[0]][0]
                q1 = ps.tiles[at_t[-1]][0] + ps.tiles[at_t[-1]][1]
                wq = load_w(w_qkv_d[hp], 4096)
                for hh in range(2):
                    p = PF()
                    for kc in range(16):
                        mm(p.ap[:, q0:q1], wq.ap[:, kc * 256 + hh * 128:kc * 256 + hh * 128 + 128], hT.ap[:, kc, q0:q1], kc == 0, kc == 15, [hT, wq], [p])
                    act(qT.ap[:, hh, q0:q1], p.ap[:, q0:q1], AF.Copy, [p], [qT], scale=float(128 ** -0.5))
            wk = load_w(w_qkv_d[8 + hp], 4096)
            wv = load_w(w_qkv_d[16 + hp], 4096)
            want_out = ((ps.kind == "s") or ps.out) and not os.environ.get("DBG_NOOUT")
            for which, w, cur, od in (("k", wk, kcur, (s_k_d if ps.kind == "s" else p_k_d)),
                                      ("v", wv, vcur, (s_v_d if ps.kind == "s" else p_v_d))):
                for t in kv_t:
                    off, n = ps.tiles[t]
                    p = PF()
                    for kc in range(16):
                        mm(p.ap[0:n, 0:256], hT.ap[:, kc, off:off + n], w.ap[:, kc * 256:(kc + 1) * 256], kc == 0, kc == 15, [hT, w], [p])
                    act(cur[t].ap[0:n, :], p.ap[0:n, 0:256], AF.Copy, [p], [cur[t]])
                    if want_out:
                        stg = kvst[(t + (which == "v")) % 2]
                        if os.environ.get("DBG_STG"):
                            stg = ngb[0]
                        if not os.environ.get("DBG_NOCP"):
                            act(stg.ap[0:n, :], p.ap[0:n, 0:256], AF.Copy, [p], [stg])
                        if not os.environ.get("DBG_NODMA"):
                            dma(os.environ.get("DBG_OQ", "sp"), od[off:off + n, hp * 256:(hp + 1) * 256], stg.ap[0:n, :], [stg], [])
                    if which == "k" and not os.environ.get("DBG_NOTR"):
                        pb = PB()
                        for hh in range(2):
                            tr(pb.ap[:, hh, 0:n], cur[t].ap[0:n, hh * 128:(hh + 1) * 128], identb.ap[0:n, 0:n], [cur[t]], [pb])
                        act(kTcur.ap[:, :, off:off + n], pb.ap[:, 0:2, 0:n], AF.Copy, [pb], [kTcur])
            if full:
                for hh in range(2):
                    h = hp * 2 + hh
                    b2 = bias2[hh]
                    base = h * (64 * LB + 64) + 63
                    srcA = bass.AP(tensor=fsc_h, offset=base, ap=[[LB - 1, 64], [1, 576]])
                    dma("sp", b2.ap[0:64, 0:576], srcA, [fsc_d], [b2])
                    dma("sp", b2.ap[64:128, 64:640], srcA, [fsc_d], [b2])
                if ps.kind == "p":
                    order = [(t, hh) for hh in range(2) for t in at_t]
                else:
                    order = [(t, hh) for t in at_t for hh in range(2)]
                ctxs = {}

                def stageA(idx, t, hh):
                    off, n = ps.tiles[t]
                    h = hp * 2 + hh
                    b2 = bias2[hh]
                    pbuf = p_bfs[idx % 2]
                    stt_ = sts[idx % 2]
                    nmask = 0
                    if ps.kind == "p":
                        blocks = []
                        for kb in range(5):
                            tt_ = t - 4 + kb
                            if ps.x0 + tt_ * 128 < 1024:
                                nmask = (kb + 1) * 128
                            if tt_ < 0:
                                blocks.append((khist.ap[:, h, (4 + tt_) * 128:(5 + tt_) * 128], [khist],
                                               vhist.ap[:, 4 + tt_, h * 128:(h + 1) * 128], [vhist], 128))
                            else:
                                blocks.append((kTcur.ap[:, hh, tt_ * 128:(tt_ + 1) * 128], [kTcur],
                                               vcur[tt_].ap[:, hh * 128:(hh + 1) * 128], [vcur[tt_]], 128))
                        nk_tot = 640
                    else:
                        if hh == 0:
                            dma("pool", kc_tm.ap[:, :, :], ck_d[t, :, hp * 256:(hp + 1) * 256].rearrange("(a p) c -> p a c", p=128), [], [kc_tm])
                            dma("pool", vc_tm.ap[:, :, :], cv_d[t, :, hp * 256:(hp + 1) * 256].rearrange("(a p) c -> p a c", p=128), [], [vc_tm])
                            pb = PB()
                            for h2 in range(2):
                                for a in range(4):
                                    tr(pb.ap[:, h2 * 4 + a, :], kc_tm.ap[:, a, h2 * 128:(h2 + 1) * 128], identb.ap[:, :], [kc_tm], [pb])
                            act(kTc.ap[:, :, :], pb.ap[:, :, :], AF.Copy, [pb], [kTc])
                        blocks = []
                        for kb in range(4):
                            blocks.append((kTc.ap[:, hh * 4 + kb, :], [kTc], vc_tm.ap[:, kb, hh * 128:(hh + 1) * 128], [vc_tm], 128))
                        blocks.append((kTcur.ap[:, hh, off:off + n], [kTcur], vcur[t].ap[0:n, hh * 128:(hh + 1) * 128], [vcur[t]], n))
                        nk_tot = 512 + n
                    c = 0
                    for (kap, kdeps, vap, vdeps, nk) in blocks:
                        mm(psc.ap[0:n, c:c + nk], qT.ap[:, hh, off:off + n], kap, True, True, [qT] + kdeps, [psc])
                        c += nk
                    tt(s_sb.ap[0:n, 0:512], psc.ap[0:n, 0:512], b2.ap[0:n, 0:512], ALU.add, [psc, b2], [s_sb])
                    tt(s_sb.ap[0:n, 512:nk_tot], psc.ap[0:n, 512:nk_tot], b2.ap[0:n, 512:nk_tot], ALU.add, [psc, b2], [s_sb])
                    if nmask:
                        ts(s_sb.ap[0:n, 0:nmask], s_sb.ap[0:n, 0:nmask], maskval[0:n, :], 0.0, ALU.add, ALU.add, [s_sb, ctab], [s_sb])
                    S.op("dve", lambda e, n=n, nk_tot=nk_tot: e.reduce_max(out=stt_.ap[0:n, 4:5], in_=s_sb.ap[0:n, 0:nk_tot], axis=AX.X), [s_sb], [stt_])
                    ts(stt_.ap[0:n, 5:6], stt_.ap[0:n, 4:5], -1.0, 0.0, ALU.mult, ALU.add, [stt_], [stt_])
                    act(pbuf.ap[0:n, 0:nk_tot], s_sb.ap[0:n, 0:nk_tot], AF.Exp, [s_sb, stt_], [pbuf, stt_], bias=stt_.ap[0:n, 5:6], accum_out=stt_.ap[0:n, 6:7])
                    ctxs[idx] = (t, hh, off, n, h, blocks, pbuf, stt_)

                def stageB(idx):
                    (t, hh, off, n, h, blocks, pbuf, stt_) = ctxs.pop(idx)
                    pb = PB()
                    c = 0
                    for kb, (kap, kdeps, vap, vdeps, nk) in enumerate(blocks):
                        tr(pb.ap[0:nk, kb, 0:n], pbuf.ap[0:n, c:c + nk], identb.ap[0:n, 0:n], [pbuf], [pb])
                        c += nk
                    nlast = blocks[-1][4]
                    if nlast == 128:
                        act(pT_sb.ap[:, 0:5, 0:n], pb.ap[:, 0:5, 0:n], AF.Copy, [pb], [pT_sb])
                    else:
                        act(pT_sb.ap[:, 0:4, 0:n], pb.ap[:, 0:4, 0:n], AF.Copy, [pb], [pT_sb])
                        act(pT_sb.ap[0:nlast, 4, 0:n], pb.ap[0:nlast, 4, 0:n], AF.Copy, [pb], [pT_sb])
                    po = PF()
                    for kb, (kap, kdeps, vap, vdeps, nk) in enumerate(blocks):
                        mm(po.ap[0:n, 0:128], pT_sb.ap[0:nk, kb, 0:n], vap, kb == 0, kb == len(blocks) - 1, [pT_sb] + vdeps, [po])
                    S.op("dve", lambda e, n=n: e.reciprocal(out=stt_.ap[0:n, 7:8], in_=stt_.ap[0:n, 6:7]), [stt_], [stt_])
                    ts(o_out[t].ap[0:n, h * 128:(h + 1) * 128], po.ap[0:n, 0:128], stt_.ap[0:n, 7:8], 1.0, ALU.mult, ALU.mult, [po, stt_], [o_out[t]])

                pipelined = ps.kind == "p"
                for idx, (t, hh) in enumerate(order):
                    if pipelined:
                        if idx == 0:
                            stageA(0, t, hh)
                        if idx + 1 < len(order):
                            stageA(idx + 1, *order[idx + 1])
                        stageB(idx)
                    else:
                        stageA(idx, t, hh)
                        stageB(idx)
            if ps.kind == "p" and not os.environ.get("DBG_NOHIST"):
                for t in kv_t:
                    off, n = ps.tiles[t]
                    for hh in range(2):
                        cp(khist.ap[:, hp * 2 + hh, off:off + n], kTcur.ap[:, hh, off:off + n], [kTcur], [khist], eng="dve")
                    act(vhist.ap[:, t, hp * 256:(hp + 1) * 256], vcur[t].ap[:, :], AF.Copy, [vcur[t]], [vhist])

    dma("sp", ctab.ap[:, :], ctab_d[:, :], [], [ctab])
    cp(identb.ap[:, :], ctab.ap[:, 256:384], [ctab], [identb])
    dma("sp", ngc.ap[:, :, :], ng_d.rearrange("g p k -> p g k"), [], [ngc])
    dma("sp", w2_sb.ap[:, :], w2_d[:, :], [], [w2_sb])
    dma("sp", gateb.ap[:, :], gateb_d[0:1, :].partition_broadcast(128), [], [gateb])
    dma("sp", ngb[0].ap[:, :], retg_d[0:1, :].partition_broadcast(128), [], [ngb[0]])
    dma("sp", ngb[1].ap[:, :], glag_d[0:1, :].partition_broadcast(128), [], [ngb[1]])
    for l in range(2):
        for w_ in range(3):
            dma("sp", cw[l].ap[:, w_, :], cw_d[l, w_].rearrange("(fb p) -> p fb", p=128), [], [cw[l]], slow=True)
        dma("sp", cbt[l].ap[:, :], cb_d[l].rearrange("(fb p) -> p fb", p=128), [], [cbt[l]], slow=True)
        S.op("dve", lambda e, l=l: e.memset(ghist[l].ap[:, :, :], 0.0), [], [ghist[l]])
        for s in range(4):
            for r in range(2):
                dma("sp", ghs_in[l][s].ap[:, :, r], st_conv_d[l, s, r].rearrange("(fb p) -> p fb", p=128), [], [ghs_in[l][s]], slow=True)
    for h in range(4):
        S.op("dve", lambda e, h=h: e.memset(Sf_ret[h].ap[:, :], 0.0), [], [Sf_ret[h]])
        S.op("dve", lambda e, h=h: e.memset(Sf_gla[h].ap[:, :], 0.0), [], [Sf_gla[h]])
    S.op("pool", lambda e: e.memset(khist.ap[:, :, :], 0.0), [], [khist])
    S.op("pool", lambda e: e.memset(vhist.ap[:, :, :], 0.0), [], [vhist])
    S.op("dve", lambda e: e.memset(acc.ap[0:16, 0:319], 0.0), [], [acc])
    dma("sp", erb.ap[:, 319:LB], rbr_d[:, 0:320], [], [erb])
    ts(erb.ap[:, 0:319], acc.ap[0:16, 0:319], erb.ap[:, 319:320], 0.0, ALU.add, ALU.add, [acc, erb], [erb])
    for r in range(64):
        dstF = bass.AP(tensor=fsc_h, offset=r * LB, ap=[[64 * LB + 64, 16], [1, LB]])
        dma("sp", dstF, erb.ap[:, :], [erb], [fsc_d])

    passes = [
        Pass("p", x0=0, l0_rest=[3], l1_kv=[3], l1_attn=[], l1_rest=None, first=True),
        Pass("p", x0=512, l1_attn=[3], l1_rest="hist"),
        Pass("p", x0=1024, maskable=True, yrow=0),
        Pass("p", x0=1536, out=True, yrow=512),
        Pass("s"),
    ]
    passes = passes[:npass] if npass < 5 else passes
    if npass < 5:
        passes[-1].out = True
    for pi, ps in enumerate(passes):
        T_ = ps.T
        src = xp_d if ps.kind == "p" else xs_d
        for t, (off, n) in enumerate(ps.xtiles):
            dma("sp", xt[t].ap[0:n, :], src[ps.x0 + off:ps.x0 + off + n, :], [], [xt[t]])
        if ps.kind == "p":
            dma("sp", rope.ap[:, :, 0:T_], rope_p_d[:, :, ps.x0:ps.x0 + T_], [], [rope])
        else:
            dma("sp", rope.ap[:, :, 0:T_], rope_s_d[:, :, :], [], [rope])
        norm_hT(ps, 0)
        l0_mixer(ps)
        if stop_after == "mixer0":
            continue
        r0 = ps.l0_rest
        o_to_hT(ps, r0)
        out_proj(ps, w_oab_d, r0)
        norm_hT(ps, 1, r0)
        ffn(ps, 0, r0)
        if stop_after == "ffn0":
            continue
        norm_hT(ps, 2, ps.l1_kv)
        if stop_after == "l1norm":
            continue
        if stop_after == "l1kv":
            ps.l1_attn = []
        l1_attn(ps)
        if ps.l1_rest is None or stop_after in ("l1kv", "attn"):
            continue
        r1 = ps.l1_attn
        o_to_hT(ps, r1)
        out_proj(ps, w_oat_d, r1)
        if stop_after == "l1out":
            continue
        norm_hT(ps, 3, r1)
        if ps.l1_rest == "hist":
            ffn(ps, 1, r1, hist_only=True)
            continue
        ffn(ps, 1, r1)
        if stop_after == "ffn1":
            continue
        yd = yp_d if ps.kind == "p" else ys_d
        cnt["w"] += 1
        gfw = wbs[cnt["w"] % NW]
        gfin_ap = gfw.ap[:, :].bitcast(F32)
        dma("sp", gfin_ap, gfin_d[0:1, :].partition_broadcast(128), [], [gfw])
        for t, (off, n) in enumerate(ps.xtiles):
            act(hb.ap[0:n, :], xt[t].ap[0:n, :], AF.Square, [xt[t]], [hb, st], accum_out=st.ap[0:n, 0:1])
            rstd_from(st.ap[0:n, 0:1], n, 1.0 / D)
            stt(xt[t].ap[0:n, :], xt[t].ap[0:n, :], st.ap[0:n, 3:4], gfin_ap[0:n, :], ALU.mult, ALU.mult, [xt[t], st, gfw], [xt[t]])
            dma("sp", yd[ps.yrow + off:ps.yrow + off + n, :], xt[t].ap[0:n, :], [xt[t]], [])
    S.final_wait_all()
    S.emit(nc)
    es.close()
    return nc


def _consts(half):
    i = np.arange(128, dtype=np.float64)
    inv = 10000.0 ** (-i / 128.0)

    def cs(pos):
        ang = inv[:, None] * np.asarray(pos, dtype=np.float64)[None, :]
        return np.stack([np.cos(ang), np.sin(ang)], axis=1).astype(np.float32)

    if half == 1:
        pos_p = np.arange(2048)
    else:
        pos_p = np.concatenate([np.zeros(1024), np.arange(1024)])
    rope_p = cs(pos_p)
    rope_s = cs(np.tile(2048 + np.arange(32), 4))
    gam = np.array([1.0 - 2.0 ** (-5 - h) for h in range(4)], dtype=np.float64)

    def dt(C, Tn):
        tpos = (np.arange(Tn) % C) + 1
        out = np.zeros((128, 8, Tn), np.float32)
        for h in range(4):
            out[:, 2 * h, :] = (gam[h] ** tpos)[None, :]
            out[:, 2 * h + 1, :] = (gam[h] ** (-tpos) * 256.0 ** -0.5)[None, :]
        return out

    ctab = np.zeros((128, 385), np.float32)
    jj, ii = np.meshgrid(np.arange(128), np.arange(128), indexing="ij")
    ctab[:, 0:128] = (ii >= jj).astype(np.float32)
    ctab[:, 128:256] = (ii >= jj).astype(np.float32) / 16.0
    ctab[:, 256:384] = np.eye(128, dtype=np.float32)
    ctab[:, 384] = 0.0 if half == 1 else NEG
    return dict(rope_p=rope_p, rope_s=rope_s, dtab_p=dt(128, 512), dtab_s=dt(32, 128), ctab=ctab)


_NC_CACHE = {}


def make_in_maps(inp):
    f = lambda a: np.ascontiguousarray(a, dtype=np.float32)
    def blk(w):
        nb = w.shape[1] // 256
        return f(w.reshape(16, 128, nb, 256).transpose(2, 1, 0, 3).reshape(nb, 128, 4096))

    def blkc(w):
        ncl = w.shape[1]
        return w.reshape(16, 128, ncl).transpose(1, 0, 2).reshape(128, 16 * ncl)

    def blkr(w, nr):
        return w.reshape(nr, 128, 512).transpose(1, 0, 2).reshape(128, nr * 512)

    w_in = inp["w_in_ab"][0]
    w_up = inp["w_ffn_up"]
    w_dn = inp["w_ffn_down"]
    up2 = np.zeros((2, 4, 5, 2, 128, 4096), np.float32)
    up1 = np.zeros((2, 4, 2, 128, 2048), np.float32)
    dnA = np.zeros((2, 4, 4, 128, 4096), np.float32)
    dnB = np.zeros((2, 4, 4, 128, 1536), np.float32)
    for l in range(2):
        for q in range(4):
            f0 = q * 11
            for i in range(5):
                c0 = (f0 + 2 * i) * 128
                up2[l, q, i, 0] = blkc(w_up[l][:, c0:c0 + 256])
                up2[l, q, i, 1] = blkc(w_up[l][:, DFF + c0:DFF + c0 + 256])
            c0 = (f0 + 10) * 128
            up1[l, q, 0] = blkc(w_up[l][:, c0:c0 + 128])
            up1[l, q, 1] = blkc(w_up[l][:, DFF + c0:DFF + c0 + 128])
            for cb in range(4):
                dnA[l, q, cb] = blkr(w_dn[l][f0 * 128:(f0 + 8) * 128, cb * 512:(cb + 1) * 512], 8)
                dnB[l, q, cb] = blkr(w_dn[l][(f0 + 8) * 128:(f0 + 11) * 128, cb * 512:(cb + 1) * 512], 3)
    shared = dict(
        w_in_blk=blk(w_in[:, 0:7168]), w_lo=f(blkc(w_in[:, 7168:7184])),
        gate_w2=f(inp["gla_gate_w2"][0]), gate_b=f(inp["gla_gate_b"]),
        ret_g=f(inp["ret_norm_g"]), gla_g=f(inp["gla_norm_g"]), w_out_ab=blk(inp["w_out_ab"][0]),
        w_qkv=blk(inp["w_qkv_att"][0]), rbr=f(inp["rel_bias_att"][0][:, ::-1]), w_out_att=blk(inp["w_out_att"][0]),
        norm_final_g=f(inp["norm_final_g"][None, :]), w_up2=up2, w_up1=up1, conv_w=f(inp["ffn_conv_w"]),
        conv_b=f(inp["ffn_conv_b"]), w_dnA=dnA, w_dnB=dnB,
    )
    ng = np.stack([inp["norm_mix_g"][0], inp["norm_ffn_g"][0], inp["norm_mix_g"][1], inp["norm_ffn_g"][1],
                   inp["norm_final_g"]], axis=0)
    shared["norm_g"] = f(ng.reshape(5, 16, 128).transpose(0, 2, 1))
    consts = [_consts(0), _consts(1)]
    maps = []
    for c in range(8):
        seq, half = c // 2, c % 2
        xs = inp["x_prompt"][seq]
        if half == 1:
            xp = f(xs)
        else:
            xp = f(np.concatenate([np.zeros((1024, D), np.float32), xs[:1024]], axis=0))
        sl = slice(4 * c, 4 * c + 4)
        m = dict(shared)
        m.update(consts[half])
        m.update(
            xp=xp, xs=f(inp["x_sample"][sl].reshape(128, D)),
            st_ret=f(inp["state_ret"][0, sl]), st_gla=f(inp["state_gla"][0, sl]),
            ck=f(inp["cache_attn_k"][0, sl].reshape(4, 512, D)), cv=f(inp["cache_attn_v"][0, sl].reshape(4, 512, D)),
            st_conv=f(inp["state_ffn_conv"][:, sl]),
        )
        maps.append(m)
    return maps


def assemble(results):
    y_prompt = np.zeros((4, 2048, D), np.float32)
    y_sample = np.zeros((32, 32, D), np.float32)
    p_ret = np.zeros((1, 4, 4, 256, 256), np.float32)
    p_gla = np.zeros((1, 4, 4, 128, 256), np.float32)
    p_k = np.zeros((1, 4, 512, 16, 128), np.float32)
    p_v = np.zeros((1, 4, 512, 16, 128), np.float32)
    p_conv = np.zeros((2, 4, 2, DFF), np.float32)
    s_ret = np.zeros((1, 32, 4, 256, 256), np.float32)
    s_gla = np.zeros((1, 32, 4, 128, 256), np.float32)
    s_k = np.zeros((1, 32, 32, 16, 128), np.float32)
    s_v = np.zeros((1, 32, 32, 16, 128), np.float32)
    s_conv = np.zeros((2, 32, 2, DFF), np.float32)
    for c in range(8):
        r = results[c]
        seq, half = c // 2, c % 2
        sl = slice(4 * c, 4 * c + 4)
        y_prompt[seq, half * 1024:(half + 1) * 1024] = r["yp"]
        y_sample[sl] = r["ys"].reshape(4, 32, D)
        s_ret[0, sl] = r["s_ret"]
        s_gla[0, sl] = r["s_gla"]
        s_k[0, sl] = r["s_k"].reshape(4, 32, 16, 128)
        s_v[0, sl] = r["s_v"].reshape(4, 32, 16, 128)
        s_conv[:, sl] = r["s_conv"]
        if half == 1:
            p_ret[0, seq] = r["p_ret"]
            p_gla[0, seq] = r["p_gla"]
            p_k[0, seq] = r["p_k"].reshape(512, 16, 128)
            p_v[0, seq] = r["p_v"].reshape(512, 16, 128)
            p_conv[:, seq] = r["p_conv"]
    return (y_prompt, y_sample, p_ret, p_gla, p_k, p_v, p_conv, s_ret, s_gla, s_k, s_v, s_conv)


def kernel(**inputs):
    inp = {k: np.asarray(v) for k, v in inputs.items()}
    if "nc" not in _NC_CACHE:
        _NC_CACHE["nc"] = build_program()
    nc = _NC_CACHE["nc"]
    maps = make_in_maps(inp)
    res = run_bass_kernel_spmd(nc, maps, core_ids=list(range(8)))
    return assemble(res.results)
```
